# Optimizing a Trainium2 kernel written in Bass

```python
import math
import jax, jax.numpy as jnp
from jax import lax
import numpy as np

D_MODEL = 1024
BATCH = 8
SEQ = 2048
DEPTH = 1

GRID_W = 64
CTX_LEN = 256
N_HEADS = 8
N_KV_HEADS = 2
HEAD_DIM = 64
GQA_REP = N_HEADS // N_KV_HEADS
ATTN_WIDTH = N_HEADS * HEAD_DIM
KV_WIDTH = N_KV_HEADS * HEAD_DIM
ATTN_SCALE = HEAD_DIM ** -0.5
Q_BLOCK = 128
ROPE_THETA = 10000.0
ROPE_FREQS = HEAD_DIM // 4
S5_WIDTH = 512
S5_GROUP = 16
S5_GROUPS = S5_WIDTH // S5_GROUP
S5_STATE = 64
DT_MIN = 1e-3
DT_MAX = 1e-1
EPS = 1e-6
IN_SIZES = (ATTN_WIDTH, KV_WIDTH, KV_WIDTH, ATTN_WIDTH, S5_WIDTH, S5_WIDTH, D_MODEL, D_MODEL)
IN_WIDTH = sum(IN_SIZES)

kernel_name = "hybrid_gqa_s5_prefix_dit_block"


def _rmsnorm(x, g):
    xf = x.astype(jnp.float32)
    y = xf * lax.rsqrt(jnp.mean(xf * xf, axis=-1, keepdims=True) + EPS)
    return (y * g.astype(jnp.float32)).astype(x.dtype)


def _adaln(cond, w, b):
    m = jax.nn.silu(cond) @ w + b
    return jnp.split(m, 3, axis=-1)


def _project(hn, w_in, q_g, k_g):
    p = hn @ w_in
    offsets = np.cumsum(IN_SIZES)[:-1].tolist()
    q, k, v, ga, u, gb, ma, mb = jnp.split(p, offsets, axis=-1)
    bsz, n = hn.shape[:2]
    q = _rmsnorm(q.reshape(bsz, n, N_HEADS, HEAD_DIM), q_g)
    k = _rmsnorm(k.reshape(bsz, n, N_KV_HEADS, HEAD_DIM), k_g)
    v = v.reshape(bsz, n, N_KV_HEADS, HEAD_DIM)
    return q, k, v, ga, u, gb, ma, mb


def _axial_angles(n):
    rows = n // GRID_W
    row_ids = jnp.repeat(jnp.arange(rows, dtype=jnp.float32), GRID_W)
    col_ids = jnp.tile(jnp.arange(GRID_W, dtype=jnp.float32), rows)
    freqs = ROPE_THETA ** (-jnp.arange(ROPE_FREQS, dtype=jnp.float32) / ROPE_FREQS)
    return row_ids[:, None] * freqs, col_ids[:, None] * freqs


def _rope_1d(x, ang):
    cos = jnp.cos(ang)[:, None, :].astype(x.dtype)
    sin = jnp.sin(ang)[:, None, :].astype(x.dtype)
    x1, x2 = jnp.split(x, 2, axis=-1)
    return jnp.concatenate([x1 * cos - x2 * sin, x2 * cos + x1 * sin], axis=-1)


def _axial_rope(x, ang_row, ang_col):
    x_row, x_col = jnp.split(x, 2, axis=-1)
    return jnp.concatenate([_rope_1d(x_row, ang_row), _rope_1d(x_col, ang_col)], axis=-1)


def _attend(q5, k, v):
    s = jnp.einsum('bqgrd,bkgd->bgrqk', q5, k).astype(jnp.float32) * ATTN_SCALE
    p = jax.nn.softmax(s, axis=-1).astype(v.dtype)
    return jnp.einsum('bgrqk,bkgd->bqgrd', p, v)


def _latent_attention(q, k_lat, v_lat, k_ctx, v_ctx):
    bsz, n = q.shape[:2]
    ang_row, ang_col = _axial_angles(n)
    q = _axial_rope(q, ang_row, ang_col)
    k_lat = _axial_rope(k_lat, ang_row, ang_col)
    k_all = jnp.concatenate([k_ctx, k_lat], axis=1)
    v_all = jnp.concatenate([v_ctx, v_lat], axis=1)
    nb = n // Q_BLOCK
    qb = q.reshape(bsz, nb, Q_BLOCK, N_KV_HEADS, GQA_REP, HEAD_DIM).transpose(1, 0, 2, 3, 4, 5)
    o = lax.map(lambda blk: _attend(blk, k_all, v_all), qb)
    return o.transpose(1, 0, 2, 3, 4, 5).reshape(bsz, n, ATTN_WIDTH)


def _context_attention(q, k, v):
    bsz, n = q.shape[:2]
    q5 = q.reshape(bsz, n, N_KV_HEADS, GQA_REP, HEAD_DIM)
    return _attend(q5, k, v).reshape(bsz, n, ATTN_WIDTH)


def _s5_discretize(lam_re, lam_im, log_dt, b_re, b_im):
    lam = lax.complex(jnp.minimum(lam_re.astype(jnp.float32), -1e-4), lam_im.astype(jnp.float32))
    dt = jnp.exp(log_dt.astype(jnp.float32))[:, None]
    lam_bar = jnp.exp(lam * dt)
    b = lax.complex(b_re.astype(jnp.float32), b_im.astype(jnp.float32))
    b_bar = ((lam_bar - 1.0) / lam)[..., None] * b
    return lam_bar, b_bar


def _ssm_combine(e1, e2):
    a1, b1 = e1
    a2, b2 = e2
    return a1 * a2, a2 * b1 + b2


def _s5_scan(u, lam_bar, b_bar, reverse, h0=None):
    bu = jnp.einsum('blgh,gph->blgp', u.astype(jnp.complex64), b_bar)
    a = jnp.broadcast_to(lam_bar, bu.shape)
    a_cum, h = lax.associative_scan(_ssm_combine, (a, bu), axis=1, reverse=reverse)
    if h0 is not None:
        h = h + a_cum * h0[:, None]
    return h


def _half_glu(y, w, b):
    z = jax.nn.gelu(y)
    return z * jax.nn.sigmoid(z @ w + b)


def _s5_branch(u_lat, u_ctx, lam_re, lam_im, log_dt, b_re, b_im, c_re, c_im, d_skip, w_glu, b_glu, with_ctx):
    bsz, n = u_lat.shape[:2]
    n_ctx = u_ctx.shape[1]
    ul = u_lat.astype(jnp.float32)
    uc = u_ctx.astype(jnp.float32)
    ul4 = ul.reshape(bsz, n, S5_GROUPS, S5_GROUP)
    uc4 = uc.reshape(bsz, n_ctx, S5_GROUPS, S5_GROUP)
    d = d_skip.astype(jnp.float32)
    y_lat = d * ul
    y_ctx = d * uc
    for direction in range(2):
        rev = direction == 1
        lam_bar, b_bar = _s5_discretize(lam_re[direction], lam_im[direction], log_dt[direction],
                                        b_re[direction], b_im[direction])
        cm = lax.complex(c_re[direction].astype(jnp.float32), c_im[direction].astype(jnp.float32))
        h_ctx = _s5_scan(uc4, lam_bar, b_bar, rev)
        h_end = h_ctx[:, 0] if rev else h_ctx[:, -1]
        h_lat = _s5_scan(ul4, lam_bar, b_bar, rev, h_end)
        y_lat = y_lat + jnp.einsum('blgp,ghp->blgh', h_lat, cm).real.reshape(bsz, n, S5_WIDTH)
        if with_ctx:
            y_ctx = y_ctx + jnp.einsum('blgp,ghp->blgh', h_ctx, cm).real.reshape(bsz, n_ctx, S5_WIDTH)
    out_lat = _half_glu(y_lat, w_glu, b_glu).astype(u_lat.dtype)
    out_ctx = _half_glu(y_ctx, w_glu, b_glu).astype(u_ctx.dtype) if with_ctx else None
    return out_lat, out_ctx


def _merge(y_attn, g_attn, y_s5, g_s5, m_attn, m_s5, w_br_a, w_br_b, w_out):
    ya = (y_attn * jax.nn.silu(g_attn)) @ w_br_a
    yb = (y_s5 * jax.nn.silu(g_s5)) @ w_br_b
    return (jax.nn.sigmoid(m_attn) * ya + jax.nn.sigmoid(m_s5) * yb) @ w_out


def setup_inputs(seed: int = 0) -> dict:
    key = jax.random.key(seed)
    ks = jax.random.split(key, 24)
    f32 = jnp.float32
    nrm = lambda k, shape, s: jax.random.normal(k, shape, f32) * s
    G, P, H = S5_GROUPS, S5_STATE, S5_GROUP
    lam_im_init = jnp.pi * jnp.arange(P, dtype=f32)
    return {
        "x": nrm(ks[0], (BATCH, SEQ, D_MODEL), 1.0),
        "c": nrm(ks[1], (BATCH, D_MODEL), 1.0),
        "ctx": nrm(ks[2], (BATCH, CTX_LEN, D_MODEL), 1.0),
        "c_ctx": nrm(ks[3], (D_MODEL,), 1.0),
        "norm_g": 1.0 + nrm(ks[4], (DEPTH, D_MODEL), 0.02),
        "w_ada": nrm(ks[5], (DEPTH, D_MODEL, 3 * D_MODEL), 0.5 * D_MODEL ** -0.5),
        "b_ada": nrm(ks[6], (DEPTH, 3 * D_MODEL), 0.02),
        "w_in": nrm(ks[7], (DEPTH, D_MODEL, IN_WIDTH), D_MODEL ** -0.5),
        "q_norm_g": 1.0 + nrm(ks[8], (DEPTH, HEAD_DIM), 0.02),
        "k_norm_g": 1.0 + nrm(ks[9], (DEPTH, HEAD_DIM), 0.02),
        "s5_lam_re": -0.5 + nrm(ks[10], (DEPTH, 2, G, P), 0.01),
        "s5_lam_im": lam_im_init + nrm(ks[11], (DEPTH, 2, G, P), 0.01),
        "s5_log_dt": jax.random.uniform(ks[12], (DEPTH, 2, G), f32, math.log(DT_MIN), math.log(DT_MAX)),
        "s5_b_re": nrm(ks[13], (DEPTH, 2, G, P, H), (2 * H) ** -0.5),
        "s5_b_im": nrm(ks[14], (DEPTH, 2, G, P, H), (2 * H) ** -0.5),
        "s5_c_re": nrm(ks[15], (DEPTH, 2, G, H, P), (2 * P) ** -0.5),
        "s5_c_im": nrm(ks[16], (DEPTH, 2, G, H, P), (2 * P) ** -0.5),
        "s5_d": nrm(ks[17], (DEPTH, S5_WIDTH), 0.5),
        "w_glu": nrm(ks[18], (DEPTH, S5_WIDTH, S5_WIDTH), S5_WIDTH ** -0.5),
        "b_glu": nrm(ks[19], (DEPTH, S5_WIDTH), 0.02),
        "w_branch_attn": nrm(ks[20], (DEPTH, ATTN_WIDTH, D_MODEL), ATTN_WIDTH ** -0.5),
        "w_branch_s5": nrm(ks[21], (DEPTH, S5_WIDTH, D_MODEL), S5_WIDTH ** -0.5),
        "w_out": nrm(ks[22], (DEPTH, D_MODEL, D_MODEL), D_MODEL ** -0.5),
        "final_norm_g": 1.0 + nrm(ks[23], (D_MODEL,), 0.02),
    }


def reference(x, c, ctx, c_ctx, norm_g, w_ada, b_ada, w_in, q_norm_g, k_norm_g,
              s5_lam_re, s5_lam_im, s5_log_dt, s5_b_re, s5_b_im, s5_c_re, s5_c_im, s5_d,
              w_glu, b_glu, w_branch_attn, w_branch_s5, w_out, final_norm_g):
    h = x
    hc = ctx
    for layer in range(DEPTH):
        with_ctx = layer + 1 < DEPTH
        shift, scale, gate = _adaln(c, w_ada[layer], b_ada[layer])
        shift_c, scale_c, gate_c = _adaln(c_ctx, w_ada[layer], b_ada[layer])
        xn = _rmsnorm(h, norm_g[layer]) * (1.0 + scale[:, None]) + shift[:, None]
        cn = _rmsnorm(hc, norm_g[layer]) * (1.0 + scale_c) + shift_c
        q, k, v, ga, u, gb, ma, mb = _project(xn, w_in[layer], q_norm_g[layer], k_norm_g[layer])
        qc, kc, vc, gac, uc, gbc, mac, mbc = _project(cn, w_in[layer], q_norm_g[layer], k_norm_g[layer])
        y_attn = _latent_attention(q, k, v, kc, vc)
        y_s5, y_s5_c = _s5_branch(u, uc, s5_lam_re[layer], s5_lam_im[layer], s5_log_dt[layer],
                                  s5_b_re[layer], s5_b_im[layer], s5_c_re[layer], s5_c_im[layer],
                                  s5_d[layer], w_glu[layer], b_glu[layer], with_ctx)
        h = h + gate[:, None] * _merge(y_attn, ga, y_s5, gb, ma, mb,
                                       w_branch_attn[layer], w_branch_s5[layer], w_out[layer])
        if with_ctx:
            y_attn_c = _context_attention(qc, kc, vc)
            hc = hc + gate_c * _merge(y_attn_c, gac, y_s5_c, gbc, mac, mbc,
                                      w_branch_attn[layer], w_branch_s5[layer], w_out[layer])
    return _rmsnorm(h, final_norm_g)
```

```python
import math
import numpy as np
import ml_dtypes
import concourse.bass as bass
import concourse.mybir as mybir
from concourse.bass_utils import run_bass_kernel_spmd
from concourse.alu_op_type import AluOpType as ALU

F32 = mybir.dt.float32
BF16 = mybir.dt.bfloat16
I32 = mybir.dt.int32
AF = mybir.ActivationFunctionType
AX = mybir.AxisListType

ENGS = ['pe', 'act', 'dve', 'pool', 'sp']
EPS = 1e-6
NLAT = 2048
NCTX = 256
NTOK = NLAT + NCTX
NCH = NTOK // 8
D = 1024


class _Op:
    __slots__ = ('eng', 'fn', 'deps', 'pos', 'needed', 'sig', 'is_dma', 'dsem', 'dval', 'waits')


class Sched:
    def __init__(self, nc, n_dma_sems=16):
        self.nc = nc
        self.sem = {e: nc.alloc_semaphore(name=f"sem_{e}") for e in ENGS}
        self.dq = {}
        for q in ('sp', 'pool'):
            self.dq[q] = dict(sems=[nc.alloc_semaphore(name=f"dsem_{q}{i}") for i in range(n_dma_sems)],
                              cnt=[0] * n_dma_sems, last=[None] * n_dma_sems, rr=0)
        self.pending = {e: [] for e in ENGS}
        self.npos = {e: 0 for e in ENGS}
        self.nsig = {e: 0 for e in ENGS}
        self.regions = {}
        self.block_dmas = []

    def _mk(self, eng, fn, is_dma):
        op = _Op()
        op.eng = eng; op.fn = fn; op.deps = []; op.needed = False; op.sig = None
        op.is_dma = is_dma; op.dsem = None; op.dval = None; op.waits = None
        op.pos = self.npos[eng]; self.npos[eng] += 1
        self.pending[eng].append(op)
        return op

    def _add_deps(self, op, reads, writes):
        deps = op.deps
        for k in reads:
            r = self.regions.get(k)
            if r is None:
                r = self.regions[k] = [None, []]
            w = r[0]
            if w is not None and (w.is_dma or op.is_dma or w.eng != op.eng or op.eng != 'pe'):
                deps.append(w)
        for k in writes:
            r = self.regions.get(k)
            if r is None:
                r = self.regions[k] = [None, []]
            w = r[0]
            if w is not None and (w.is_dma or op.is_dma or w.eng != op.eng or op.eng != 'pe'):
                deps.append(w)
            for rd in r[1]:
                if rd is not op and (rd.is_dma or op.is_dma or rd.eng != op.eng or op.eng != 'pe'):
                    deps.append(rd)
        for k in reads:
            self.regions[k][1].append(op)
        for k in writes:
            r = self.regions[k]
            r[0] = op; r[1] = []

    def op(self, eng, fn, reads=(), writes=()):
        psr = [k for k in reads if isinstance(k, tuple) and k[0] == 'ps']
        if psr:
            reads = [k for k in reads if not (isinstance(k, tuple) and k[0] == 'ps')]
            writes = list(writes) + psr
        op = self._mk(eng, fn, False)
        self._add_deps(op, reads, writes)
        return op

    def dma(self, q, fn, reads=(), writes=()):
        op = self._mk(q, fn, True)
        d = self.dq[q]
        i = d['rr']; d['rr'] = (i + 1) % len(d['sems'])
        prev = d['last'][i]
        if prev is not None:
            op.deps.append(prev)
        d['cnt'][i] += 1
        op.dsem = d['sems'][i]; op.dval = 16 * d['cnt'][i]
        d['last'][i] = op
        self._add_deps(op, reads, writes)
        self.block_dmas.append(op)
        return op

    def flush(self):
        nc = self.nc
        tail = {}
        for e in ENGS:
            waited = {}
            for op in self.pending[e]:
                ws = []
                for dp in op.deps:
                    if dp.is_dma:
                        key = ('d', id(dp.dsem))
                        if waited.get(key, 0) >= dp.dval:
                            continue
                        waited[key] = dp.dval
                        ws.append(dp)
                    else:
                        key = ('c', dp.eng)
                        if waited.get(key, -1) >= dp.pos:
                            continue
                        waited[key] = dp.pos
                        dp.needed = True
                        ws.append(dp)
                op.waits = ws
            tail[e] = waited
        for e in ENGS:
            for op in self.pending[e]:
                if (not op.is_dma) and op.needed:
                    self.nsig[e] += 1
                    op.sig = self.nsig[e]
        engobj = {'pe': 'tensor', 'act': 'scalar', 'dve': 'vector', 'pool': 'gpsimd', 'sp': 'sync'}
        sched = self

        def replay(e, eng):
            for op in sched.pending[e]:
                for dp in op.waits:
                    if dp.is_dma:
                        eng.wait_ge(dp.dsem, dp.dval)
                    else:
                        eng.wait_ge(sched.sem[dp.eng], dp.sig)
                ins = op.fn(eng)
                if op.is_dma:
                    ins.then_inc(op.dsem, 16)
                elif op.needed:
                    ins.then_inc(sched.sem[e], 1)
            for op in sched.block_dmas:
                if op.eng == e:
                    key = ('d', id(op.dsem))
                    if tail[e].get(key, 0) < op.dval:
                        tail[e][key] = op.dval
                        eng.wait_ge(op.dsem, op.dval)

        with nc.Block() as block:
            for e in ENGS:
                if not sched.pending[e]:
                    continue
                getattr(block, engobj[e])(lambda eng, e=e: replay(e, eng))
        self.pending = {e: [] for e in ENGS}
        self.regions = {}
        self.block_dmas = []


class Arena:
    def __init__(self, nc, lo, hi):
        self.nc = nc; self.lo = lo; self.hi = hi; self.cur = lo; self.n = 0; self.peak = lo

    def alloc(self, name, shape, dt):
        esz = {F32: 4, BF16: 2, I32: 4}[dt]
        nbytes = esz * int(np.prod(shape[1:]))
        off = (self.cur + 63) // 64 * 64
        assert off + nbytes <= self.hi, f"SBUF arena overflow at {name}: {off + nbytes} > {self.hi}"
        self.cur = off + nbytes
        self.peak = max(self.peak, self.cur)
        self.n += 1
        self.offs = getattr(self, 'offs', {}); self.offs[name] = off
        return self.nc.alloc_sbuf_tensor_at(f"{name}_{self.n}", list(shape), dt, offset=off)

    def mark(self):
        return self.cur

    def release(self, m):
        self.cur = m


def _bf(a):
    return np.ascontiguousarray(a.astype(ml_dtypes.bfloat16))


def _constants():
    c = {}
    c["ident_f"] = np.eye(128, dtype=np.float32)
    c["ident_b"] = _bf(np.eye(128, dtype=np.float32))
    n = NLAT
    rows = n // 64
    row_ids = np.repeat(np.arange(rows, dtype=np.float32), 64)
    col_ids = np.tile(np.arange(64, dtype=np.float32), rows)
    freqs = (np.float32(10000.0) ** (-np.arange(16, dtype=np.float32) / np.float32(16))).astype(np.float32)
    ar = (row_ids[:, None] * freqs).astype(np.float32)
    ac = (col_ids[:, None] * freqs).astype(np.float32)
    cr, sr = np.cos(ar).astype(np.float32), np.sin(ar).astype(np.float32)
    cc, sc = np.cos(ac).astype(np.float32), np.sin(ac).astype(np.float32)
    C64 = np.concatenate([cr, cr, cc, cc], axis=1)
    S64 = np.concatenate([-sr, sr, -sc, sc], axis=1)
    Cf = np.ones((NTOK, 64), np.float32); Sf = np.zeros((NTOK, 64), np.float32)
    Cf[:n] = C64; Sf[:n] = S64
    c["ropeC"] = np.ascontiguousarray(Cf.reshape(18, 128, 64).transpose(1, 0, 2))
    c["ropeS"] = np.ascontiguousarray(Sf.reshape(18, 128, 64).transpose(1, 0, 2))
    sel = np.zeros((128, 8, 8, 128), np.float32)
    for tok in range(128):
        for j in range(8):
            sel[tok, j, tok % 8, 16 * j + tok // 8] = 1.0
    c["sel"] = _bf(sel)
    selT = np.zeros((128, 8, 512), np.float32)
    for cc_ in range(128):
        for s in range(8):
            selT[cc_, s, 8 * (cc_ % 64) + s] = 1.0
    c["selT"] = _bf(selT)
    s_idx = np.arange(128) // 16
    c["maskF"] = (s_idx[None, :] >= s_idx[:, None]).astype(np.float32)
    c["maskB"] = (s_idx[None, :] <= s_idx[:, None]).astype(np.float32)
    col = np.arange(NCH)
    posF = np.where(col < 256, col + 32, col - 256).astype(np.float32)
    posB = (NCH - 1 - col).astype(np.float32)
    c["pos"] = np.ascontiguousarray(np.broadcast_to(np.stack([posF, posB])[None], (128, 2, NCH))).astype(np.float32)
    return c


def _layout_shared(inp):
    f = lambda a: np.ascontiguousarray(np.asarray(a, dtype=np.float32))
    s = {}
    s["w_ada"] = f(inp["w_ada"][0])
    s["b_adaT"] = f(inp["b_ada"][0].reshape(24, 128).T)
    s["b_gate_bc"] = f(np.broadcast_to(inp["b_ada"][0][2048:3072][None], (128, 1024)))
    s["norm_gT"] = f(inp["norm_g"][0].reshape(8, 128).T)
    s["w_in"] = f(inp["w_in"][0])
    s["qg_bc"] = f(np.broadcast_to(inp["q_norm_g"][0][None], (128, 64)))
    s["kg_bc"] = f(np.broadcast_to(inp["k_norm_g"][0][None], (128, 64)))

    def gq(a):
        a = np.asarray(a, np.float32)
        rest = a.shape[3:]
        a = a.reshape((2, 16, 2, 64) + rest)
        a = np.moveaxis(a, (2, 3), (0, 1))
        return f(a.reshape((128, 32) + rest))
    s["lamre"] = gq(inp["s5_lam_re"][0])
    s["lamim"] = gq(inp["s5_lam_im"][0])
    s["logdt"] = gq(np.broadcast_to(np.asarray(inp["s5_log_dt"][0])[:, :, None], (2, 32, 64)))
    s["bre"] = gq(inp["s5_b_re"][0])
    s["bim"] = gq(inp["s5_b_im"][0])
    s["cre"] = gq(np.transpose(np.asarray(inp["s5_c_re"][0]), (0, 1, 3, 2)))
    s["cim"] = gq(np.transpose(np.asarray(inp["s5_c_im"][0]), (0, 1, 3, 2)))
    d = np.asarray(inp["s5_d"][0], np.float32).reshape(32, 16)
    s["dcol"] = f(np.tile(d.T, (8, 1)))
    s["w_glu"] = f(inp["w_glu"][0])
    s["b_gluT"] = f(inp["b_glu"][0].reshape(4, 128).T)
    s["w_bra"] = f(inp["w_branch_attn"][0])
    s["w_brb"] = f(inp["w_branch_s5"][0])
    s["w_out"] = f(inp["w_out"][0])
    s["fg_bc"] = f(np.broadcast_to(np.asarray(inp["final_norm_g"])[None], (128, 1024)))
    return s


_SPECS = None


def build(dbg=(), stop=None):
    nc = bass.Bass('TRN2', target_bir_lowering=False)
    S = Sched(nc)
    consts = _constants()
    din = {}

    def dram_in(name, shape, dt=F32):
        din[name] = nc.dram_tensor(name, list(shape), dt, kind="ExternalInput").ap()
        return din[name]

    x_d = dram_in("x", [NLAT, D]); ctx_d = dram_in("ctx", [NCTX, D]); cT_d = dram_in("cT", [128, 8, 2])
    shared_shapes = dict(w_ada=[1024, 3072], b_adaT=[128, 24], b_gate_bc=[128, 1024], norm_gT=[128, 8],
                         w_in=[1024, 4352], qg_bc=[128, 64], kg_bc=[128, 64], lamre=[128, 32], lamim=[128, 32],
                         logdt=[128, 32], bre=[128, 32, 16], bim=[128, 32, 16], cre=[128, 32, 16], cim=[128, 32, 16],
                         dcol=[128, 32], w_glu=[512, 512], b_gluT=[128, 4], w_bra=[512, 1024], w_brb=[512, 1024],
                         w_out=[1024, 1024], fg_bc=[128, 1024])
    for k, shp in shared_shapes.items():
        dram_in(k, shp)
    for k, v in consts.items():
        dram_in(k, v.shape, BF16 if v.dtype == ml_dtypes.bfloat16 else F32)
    out_d = nc.dram_tensor("out", [NLAT, D], F32, kind="ExternalOutput").ap()
    dbg_out = {}

    def dump(name, tens_ap, shape, dt=F32, reads=()):
        if name not in dbg:
            return
        o = nc.dram_tensor("dbg_" + name, list(shape), dt, kind="ExternalOutput").ap()
        dbg_out[name] = o
        S.dma('sp', lambda e: e.dma_start(out=o, in_=tens_ap), reads=list(reads))

    PSP = [nc.alloc_psum_tensor(f"psp{i}", [128, 1024], F32) for i in range(4)]
    PS = [PSP[i // 2][:, (i % 2) * 512:(i % 2 + 1) * 512] for i in range(8)]
    AR = Arena(nc, 16512, 229344)
    HI_WTM = 229344 - 20480
    HI_WGA = HI_WTM - 8192
    wtm = nc.alloc_sbuf_tensor_at("wtm_hi", [128, 8, 1280], BF16, offset=HI_WTM)
    wga = nc.alloc_sbuf_tensor_at("wga_hi", [128, 8, 512], BF16, offset=HI_WGA)
    AR.hi = HI_WGA

    ident_f = AR.alloc("ident_f", [128, 128], F32)
    ident_b = AR.alloc("ident_b", [128, 128], BF16)
    mod = AR.alloc("mod", [128, 24, 2], F32)
    modA = AR.alloc("modA", [128, 2, 8], F32)
    gate_bc = AR.alloc("gate_bc", [128, 1024], F32)
    nhalf = AR.alloc("nhalf", [128, 16], F32)
    xnT = AR.alloc("xnT", [128, 8, NTOK], BF16)
    S.dma('sp', lambda e: e.dma_start(out=ident_f[:], in_=din["ident_f"]), writes=['ident_f'])
    S.op('pool', lambda e: e.memset(nhalf[:], -0.5), writes=['nhalf'])
    S.dma('sp', lambda e: e.dma_start(out=ident_b[:], in_=din["ident_b"]), writes=['ident_b'])
    m_persist = AR.mark()

    scT = AR.alloc("scT", [128, 8, 2], F32)
    screp = AR.alloc("screp", [128, 8, 128], F32)
    b_adaT = AR.alloc("b_adaT", [128, 24], F32)
    norm_gT = AR.alloc("norm_gT", [128, 8], F32)
    bgate = AR.alloc("bgate", [128, 1024], F32)
    wa = [AR.alloc(f"wa{i}", [128, 3072], F32) for i in range(3)]
    S.dma('sp', lambda e: e.dma_start(out=scT[:], in_=din["cT"]), writes=['scT'])
    S.dma('sp', lambda e: e.dma_start(out=b_adaT[:], in_=din["b_adaT"]), writes=['b_adaT'])
    S.dma('sp', lambda e: e.dma_start(out=norm_gT[:], in_=din["norm_gT"]), writes=['norm_gT'])
    S.dma('sp', lambda e: e.dma_start(out=bgate[:], in_=din["b_gate_bc"]), writes=['bgate'])
    S.op('act', lambda e: e.activation(out=scT[:], in_=scT[:], func=AF.Silu), reads=['scT'], writes=['scT'])
    S.op('dve', lambda e: e.tensor_copy(out=screp[:], in_=scT[:, :, 0:1].to_broadcast([128, 8, 128])),
         reads=['scT'], writes=['screp'])
    xt = [AR.alloc(f"xt{i}", [128, 1024], F32) for i in range(18)]
    junk = AR.alloc("junkB", [128, 1024], F32)
    ssB = AR.alloc("ssB", [128, 18], F32)
    rsB = AR.alloc("rsB", [128, 18], F32)

    def b_stage1(i):
        t = xt[i]
        src = x_d[i * 128:(i + 1) * 128, :] if i < 16 else ctx_d[(i - 16) * 128:(i - 15) * 128, :]
        S.dma('pool', lambda e: e.dma_start(out=t[:], in_=src), writes=[('xt', i)])
        S.op('act', lambda e: e.activation(out=junk[:], in_=t[:], func=AF.Square, accum_out=ssB[:, i:i + 1]),
             reads=[('xt', i)], writes=['junkB', ('ssB', i)])
        S.op('dve', lambda e: e.tensor_scalar(out=rsB[:, i:i + 1], in0=ssB[:, i:i + 1], scalar1=1.0 / D, scalar2=EPS,
                                              op0=ALU.mult, op1=ALU.add), reads=[('ssB', i)], writes=[('rsB', i)])

    def b_stage2(i):
        t = xt[i]
        S.op('pool', lambda e: e.tensor_tensor(out=rsB[:, i:i + 1], in0=rsB[:, i:i + 1], in1=nhalf[:, 0:1], op=ALU.pow),
             reads=[('rsB', i), 'nhalf'], writes=[('rsB', i)])
        S.op('act', lambda e: e.activation(out=t[:], in_=t[:], func=AF.Copy, scale=rsB[:, i:i + 1]),
             reads=[('xt', i), ('rsB', i)], writes=[('xt', i)])
    b_order = []
    for i in range(18):
        b_order.append((1, i))
        if i >= 1:
            b_order.append((2, i - 1))
    b_order.append((2, 17))
    b_pos = [0]

    def b_emit(n):
        for _ in range(n):
            if b_pos[0] < len(b_order):
                st, i = b_order[b_pos[0]]; b_pos[0] += 1
                (b_stage1 if st == 1 else b_stage2)(i)
    for kt in range(8):
        w = wa[kt % 3]
        if kt >= 5:
            b_emit(12)
        S.dma('sp', lambda e, w=w, kt=kt: e.dma_start(out=w[:], in_=din["w_ada"][kt * 128:(kt + 1) * 128, :]),
              writes=[('wa', kt % 3)])
        pa = PS[kt % 2]
        for j in range(24):
            S.op('pe', lambda e, w=w, pa=pa, j=j, kt=kt: e.matmul(
                pa[:, 2 * j:2 * j + 2], lhsT=w[:, j * 128:(j + 1) * 128], rhs=scT[:, kt, :], start=True, stop=True),
                reads=[('wa', kt % 3), 'scT'], writes=[('psA', kt % 2)])
        for c2 in range(2):
            S.op('pe', lambda e, w=w, c2=c2, kt=kt: e.matmul(
                PS[2 + c2][:, :], lhsT=screp[:, kt, :], rhs=w[:, 2048 + c2 * 512:2048 + (c2 + 1) * 512],
                start=(kt == 0), stop=(kt == 7)),
                reads=[('wa', kt % 3), 'screp'], writes=[('psG', c2)])
        modv = mod[:].rearrange("p j n -> p (j n)")
        if kt == 0:
            S.op('dve', lambda e, pa=pa: e.tensor_tensor(
                out=mod[:], in0=pa[:, 0:48].rearrange("p (j n) -> p j n", n=2),
                in1=b_adaT[:].unsqueeze(2).to_broadcast([128, 24, 2]), op=ALU.add),
                reads=[('psA', 0), 'b_adaT'], writes=['mod'])
        else:
            S.op('dve', lambda e, pa=pa, modv=modv: e.tensor_tensor(out=modv, in0=pa[:, 0:48], in1=modv, op=ALU.add),
                 reads=[('psA', kt % 2), 'mod'], writes=['mod'])
    for c2 in range(2):
        S.op('dve', lambda e, c2=c2: e.tensor_tensor(out=gate_bc[:, c2 * 512:(c2 + 1) * 512], in0=PS[2 + c2][:, :],
                                                     in1=bgate[:, c2 * 512:(c2 + 1) * 512], op=ALU.add),
             reads=[('psG', c2), 'bgate'], writes=[('gate_bc', c2)])
    for n in range(2):
        S.op('dve', lambda e, n=n: e.scalar_tensor_tensor(out=modA[:, n, :], in0=mod[:, 8:16, n], scalar=1.0,
                                                          in1=norm_gT[:], op0=ALU.add, op1=ALU.mult),
             reads=['mod', 'norm_gT'], writes=['modA'])
    b_emit(100)
    for kt in range(8):
        S.dma('pool', lambda e, kt=kt: e.dma_start(out=wtm[:, kt, 0:768], in_=din["w_in"][kt * 128:(kt + 1) * 128, 0:768]),
              writes=[('wtm', kt)])
        S.dma('pool', lambda e, kt=kt: e.dma_start(out=wtm[:, kt, 768:1280], in_=din["w_in"][kt * 128:(kt + 1) * 128, 1280:1792]),
              writes=[('wtm', kt)])
    dump("mod", mod[:], [128, 24, 2], reads=['mod'])
    dump("gate_bc", gate_bc[:], [128, 1024], reads=[('gate_bc', 0), ('gate_bc', 1)])

    for i in range(18):
        t = xt[i]
        n = 0 if i < 16 else 1
        for half in range(2):
            bk = (i % 2) * 2 + half + 4
            pb = PS[bk]
            for k4 in range(4):
                kt = half * 4 + k4
                S.op('pe', lambda e, t=t, pb=pb, k4=k4, kt=kt: e.transpose(
                    out=pb[:, k4 * 128:(k4 + 1) * 128], in_=t[:, kt * 128:(kt + 1) * 128], identity=ident_f[:]),
                    reads=[('xt', i), 'ident_f'], writes=[('ps', bk)])
            for k4 in range(4):
                kt = half * 4 + k4
                S.op('dve', lambda e, pb=pb, k4=k4, kt=kt, i=i, n=n: e.tensor_scalar(
                    out=xnT[:, kt, i * 128:(i + 1) * 128], in0=pb[:, k4 * 128:(k4 + 1) * 128],
                    scalar1=modA[:, n, kt:kt + 1], scalar2=mod[:, kt, n:n + 1], op0=ALU.mult, op1=ALU.add),
                    reads=[('ps', bk), 'modA', 'mod'], writes=[('xnT', kt, i)])
    dump("xnT", xnT[:], [128, 8, NTOK], BF16, reads=[('xnT', kt, i) for kt in range(8) for i in range(18)])
    S.flush()
    AR.release(m_persist)
    if stop == 'B':
        return nc, din, consts

    ygT = AR.alloc("ygT", [128, 4, NLAT], BF16)
    utm = AR.alloc("utm", [128, 18, 512], BF16)
    m_mid2 = AR.mark()
    qT = AR.alloc("qT", [128, 4, NLAT], BF16)
    kTz = AR.alloc("kTz", [128, 2, 2, NTOK], BF16)
    Vaug = AR.alloc("Vaug", [128, 18, 2, 65], BF16)
    m_mid = AR.mark()
    ropeC = AR.alloc("ropeC", [128, 18, 64], F32)
    ropeS = AR.alloc("ropeS", [128, 18, 64], F32)
    qg = AR.alloc("qg", [128, 64], F32)
    kg = AR.alloc("kg", [128, 64], F32)
    GCq = AR.alloc("GCq", [128, 18, 64], F32); GSq = AR.alloc("GSq", [128, 18, 64], F32)
    GCk = AR.alloc("GCk", [128, 18, 64], F32); GSk = AR.alloc("GSk", [128, 18, 64], F32)
    sqs_ = [AR.alloc(f"sqs{i}", [128, 640], F32) for i in range(2)]
    t1_ = [AR.alloc(f"t1{i}", [128, 640], F32) for i in range(2)]
    t2_ = [AR.alloc(f"t2{i}", [128, 640], F32) for i in range(2)]
    ssq = AR.alloc("ssq", [128, 18, 10], F32)
    qtok = [AR.alloc(f"qtok{i}", [128, 512], BF16) for i in range(2)]
    kpad = [AR.alloc(f"kpad{i}", [128, 2, 2, 2, 64], BF16) for i in range(2)]
    for kp_ in kpad:
        S.op('pool', lambda e, kp_=kp_: e.memset(kp_[:], 0.0), writes=[('kpad_init', id(kp_))])
    for kt in range(8):
        S.dma('pool', lambda e, kt=kt: e.dma_start(out=wga[:, kt, :], in_=din["w_in"][kt * 128:(kt + 1) * 128, 768:1280]),
              writes=[('wga', kt)])
    S.dma('sp', lambda e: e.dma_start(out=ropeC[:], in_=din["ropeC"]), writes=['ropeC'])
    S.dma('sp', lambda e: e.dma_start(out=ropeS[:], in_=din["ropeS"]), writes=['ropeS'])
    S.dma('sp', lambda e: e.dma_start(out=qg[:], in_=din["qg_bc"]), writes=['qg'])
    S.dma('sp', lambda e: e.dma_start(out=kg[:], in_=din["kg_bc"]), writes=['kg'])
    S.op('pool', lambda e: e.memset(Vaug[:], 1.0), writes=['Vaug_init'])
    for (g_, GC_, GS_, nm) in ((qg, GCq, GSq, 'q'), (kg, GCk, GSk, 'k')):
        S.op('pool', lambda e, g_=g_, GC_=GC_: e.tensor_tensor(out=GC_[:], in0=ropeC[:], in1=g_[:].unsqueeze(1).to_broadcast([128, 18, 64]),
                                                              op=ALU.mult), reads=['ropeC', nm + 'g'], writes=['GC' + nm])
        gv = g_[:].rearrange("p (r j f) -> p r j f", r=2, j=2)
        Sv = ropeS[:].rearrange("p i (r j f) -> p i r j f", r=2, j=2)
        Gv = GS_[:].rearrange("p i (r j f) -> p i r j f", r=2, j=2)
        for j in range(2):
            S.op('pool', lambda e, j=j, gv=gv, Sv=Sv, Gv=Gv: e.tensor_tensor(
                out=Gv[:, :, :, j, :], in0=Sv[:, :, :, j, :],
                in1=gv[:, :, 1 - j, :].unsqueeze(1).to_broadcast([128, 18, 2, 16]), op=ALU.mult),
                reads=['ropeS', nm + 'g'], writes=[('GS' + nm, j)])

    import os as _os
    _cs = _os.environ.get("CSTOP", "")
    if _cs == "setup":
        dump("GSq", GSq[:], [128, 18, 64], reads=[('GSq', 0), ('GSq', 1)])
        S.flush()
        return nc, din, consts

    def head_norm_rope(i, pv, bk, nh, GC_, GS_, nm, out3, so):
        W = nh * 64
        sl = slice(so // 64, so // 64 + nh)
        par = i % 2
        sqs = sqs_[par]; t1 = t1_[par]; t2 = t2_[par]
        nm_ = nm
        nm = (nm_, par)
        S.op('act', lambda e: e.activation(out=sqs[:, so:so + W], in_=pv, func=AF.Square), reads=[('ps', bk)], writes=[('sqs', nm)])
        yield
        p3 = pv.rearrange("p (h d) -> p h d", d=64)
        S.op('dve', lambda e: e.tensor_tensor(out=t1[:, so:so + W].rearrange("p (h d) -> p h d", d=64), in0=p3,
                                              in1=GC_[:, i, :].unsqueeze(1).to_broadcast([128, nh, 64]), op=ALU.mult),
             reads=[('ps', bk), 'GC' + nm_], writes=[('t1', nm)])
        yield
        S.op('dve', lambda e: e.tensor_reduce(out=ssq[:, i, sl], in_=sqs[:, so:so + W].rearrange("p (h d) -> p h d", d=64),
                                              axis=AX.X, op=ALU.add), reads=[('sqs', nm)], writes=[('ssq', i, nm)])
        yield
        S.op('dve', lambda e: e.tensor_scalar(out=ssq[:, i, sl], in0=ssq[:, i, sl], scalar1=1.0 / 64, scalar2=EPS,
                                              op0=ALU.mult, op1=ALU.add), reads=[('ssq', i, nm)], writes=[('ssq', i, nm)])
        yield
        S.op('pool', lambda e: e.tensor_tensor(out=ssq[:, i, sl], in0=ssq[:, i, sl], in1=nhalf[:, 0:nh], op=ALU.pow),
             reads=[('ssq', i, nm)], writes=[('ssq', i, nm)])
        yield
        p5 = pv.rearrange("p (h r j f) -> p h r j f", r=2, j=2, f=16)
        t5 = t2[:, so:so + W].rearrange("p (h r j f) -> p h r j f", r=2, j=2, f=16)
        G5 = GS_[:, i, :].rearrange("p (r j f) -> p r j f", r=2, j=2)
        for j in range(2):
            S.op('dve', lambda e, j=j: e.tensor_tensor(
                out=t5[:, :, :, j, :], in0=p5[:, :, :, 1 - j, :],
                in1=G5[:, :, j, :].unsqueeze(1).to_broadcast([128, nh, 2, 16]), op=ALU.mult),
                reads=[('ps', bk), ('GS' + nm_, 0), ('GS' + nm_, 1)], writes=[('t2', nm, j)])
            yield
        S.op('pool', lambda e: e.tensor_tensor(out=t1[:, so:so + W], in0=t1[:, so:so + W], in1=t2[:, so:so + W], op=ALU.add),
             reads=[('t1', nm), ('t2', nm, 0), ('t2', nm, 1)], writes=[('t1', nm)])
        yield
        S.op('dve', lambda e: e.tensor_tensor(out=out3, in0=t1[:, so:so + W].rearrange("p (h d) -> p h d", d=64),
                                              in1=ssq[:, i, sl].unsqueeze(2).to_broadcast([128, nh, 64]), op=ALU.mult),
             reads=[('t1', nm), ('ssq', i, nm)], writes=[(nm_ + 'tok', i % 2)])
        yield

    def tileC_mm(i):
        lat = i < 16
        s4 = (i % 2) * 4
        bq, bkv, bu, bt = s4, s4 + 1, s4 + 2, s4 + 3
        for kt in range(8):
            lhs = xnT[:, kt, i * 128:(i + 1) * 128]
            if lat:
                S.op('pe', lambda e, kt=kt, lhs=lhs: e.matmul(PS[bq][:, :], lhsT=lhs, rhs=wtm[:, kt, 0:512], start=(kt == 0), stop=(kt == 7)),
                     reads=[('xnT', kt, i), ('wtm', kt)], writes=[('ps', bq)])
            S.op('pe', lambda e, kt=kt, lhs=lhs: e.matmul(PS[bkv][:, 0:256], lhsT=lhs, rhs=wtm[:, kt, 512:768], start=(kt == 0), stop=(kt == 7)),
                 reads=[('xnT', kt, i), ('wtm', kt)], writes=[('ps', bkv)])
            S.op('pe', lambda e, kt=kt, lhs=lhs: e.matmul(PS[bu][:, :], lhsT=lhs, rhs=wtm[:, kt, 768:1280], start=(kt == 0), stop=(kt == 7)),
                 reads=[('xnT', kt, i), ('wtm', kt)], writes=[('ps', bu)])

    def tileC(i):
        lat = i < 16
        s4 = (i % 2) * 4
        bq, bkv, bu, bt = s4, s4 + 1, s4 + 2, s4 + 3
        kd = kpad[i % 2]
        gens = []
        if lat:
            gens.append(head_norm_rope(i, PS[bq][:, :], bq, 8, GCq, GSq, 'q', qtok[i % 2][:].rearrange("p (h d) -> p h d", d=64), 0))
        gens.append(head_norm_rope(i, PS[bkv][:, 0:128], bkv, 2, GCk, GSk, 'k', kd[:, :, 0, 0, :], 512))
        while gens:
            for g_ in list(gens):
                try:
                    next(g_)
                except StopIteration:
                    gens.remove(g_)
        S.op('pool', lambda e, kd=kd: e.tensor_copy(out=kd[:, :, 1, 1, :], in_=kd[:, :, 0, 0, :]),
             reads=[('ktok', i % 2), ('kpad_init', id(kd))], writes=[('kdup', i % 2)])
        S.op('act', lambda e, i=i: e.activation(out=Vaug[:, i, :, 0:64], in_=PS[bkv][:, 128:256].rearrange("p (h d) -> p h d", d=64),
                                                func=AF.Copy), reads=[('ps', bkv), 'Vaug_init'], writes=[('Vaug', i)])
        S.op('act', lambda e, i=i: e.activation(out=utm[:, i, :], in_=PS[bu][:, :], func=AF.Copy), reads=[('ps', bu)], writes=[('utm', i)])
        ptb = PS[bt][:].bitcast(BF16)
        if lat:
            for hp in range(4):
                S.op('pe', lambda e, hp=hp, i=i, ptb=ptb: e.transpose(out=ptb[:, hp * 128:(hp + 1) * 128],
                                                                      in_=qtok[i % 2][:, hp * 128:(hp + 1) * 128], identity=ident_b[:]),
                     reads=[('qtok', i % 2), 'ident_b'], writes=[('ps', bt)])
        for kv in range(2):
            for h2 in range(2):
                S.op('pe', lambda e, kv=kv, h2=h2, kd=kd, ptb=ptb: e.transpose(
                    out=ptb[:, 512 + (kv * 2 + h2) * 128:512 + (kv * 2 + h2 + 1) * 128],
                    in_=kd[:, kv, h2, :, :].rearrange("p a d -> p (a d)"), identity=ident_b[:]),
                    reads=[('ktok', i % 2), ('kdup', i % 2), 'ident_b', ('kpad_init', id(kd))], writes=[('ps', bt)])
        if lat:
            S.op('act', lambda e, i=i, ptb=ptb: e.activation(out=qT[:, :, i * 128:(i + 1) * 128],
                                                             in_=ptb[:, 0:512].rearrange("p (a t) -> p a t", t=128), func=AF.Copy),
                 reads=[('ps', bt)], writes=[('qT', i)])
        S.op('dve', lambda e, i=i, ptb=ptb: e.tensor_copy(out=kTz[:, :, :, i * 128:(i + 1) * 128],
                                                         in_=ptb[:, 512:1024].rearrange("p (a b t) -> p a b t", a=2, t=128)),
             reads=[('ps', bt)], writes=[('kTz', i)])
    tileC_mm(0)
    for i in range(18):
        if i + 1 < 18:
            tileC_mm(i + 1)
        tileC(i)
    dump("qT", qT[:], [128, 4, NLAT], BF16, reads=[('qT', i) for i in range(16)])
    dump("Vaug", Vaug[:], [128, 18, 2, 65], BF16, reads=[('Vaug', i) for i in range(18)])
    dump("utm", utm[:], [128, 18, 512], BF16, reads=[('utm', i) for i in range(18)])
    S.flush()
    AR.release(m_mid)
    if stop == 'C':
        return nc, din, consts

    sgT = AR.alloc("sgT", [128, 8, NLAT], BF16)
    pT2 = [AR.alloc(f"pT{i}", [128, 1024], BF16) for i in range(3)]
    osb = [AR.alloc(f"osb{i}", [128, 512], F32) for i in range(2)]
    rsum = AR.alloc("rsum", [128, 512], F32)
    tmpD = AR.alloc("tmpD", [128, 512], F32)
    ones_f = AR.alloc("ones_f", [128, 64], F32)
    S.op('pool', lambda e: e.memset(ones_f[:], 1.0), writes=['ones_f'])
    for hp in range(4):
        for tb in range(4):
            bk = (hp * 4 + tb) % 2
            for kt in range(8):
                S.op('pe', lambda e, kt=kt, hp=hp, tb=tb, bk=bk: e.matmul(
                    PS[bk][:, :], lhsT=wga[:, kt, hp * 128:(hp + 1) * 128], rhs=xnT[:, kt, tb * 512:(tb + 1) * 512],
                    start=(kt == 0), stop=(kt == 7)), reads=[('wga', kt)], writes=[('ps', bk)])
            for h2 in range(2):
                S.op('act', lambda e, hp=hp, tb=tb, bk=bk, h2=h2: e.activation(
                    out=sgT[0:64, 2 * hp + h2, tb * 512:(tb + 1) * 512], in_=PS[bk][h2 * 64:(h2 + 1) * 64, :], func=AF.Silu),
                    reads=[('ps', bk)], writes=[('sgT', 2 * hp + h2, tb)])
    PI = math.pi
    hp_cur = [HI_WTM]

    def HP(name, shape):
        nbytes = 4 * int(np.prod(shape[1:]))
        off = (hp_cur[0] + 63) // 64 * 64
        hp_cur[0] = off + nbytes
        assert hp_cur[0] <= 229344
        return nc.alloc_sbuf_tensor_at(name + "_hp", list(shape), F32, offset=off)
    rho = HP("rho", [128, 32]); phi = HP("phi", [128, 32])
    chain_ops = []
    S.op = lambda eng, fn, reads=(), writes=(): chain_ops.append((eng, fn, list(reads), list(writes)))
    def small(nm):
        return AR.alloc(nm, [128, 32], F32)
    lamre = small("lamre"); lamim = small("lamim"); logdt = small("logdt")
    breT = AR.alloc("breT", [128, 32, 16], F32); bimT = AR.alloc("bimT", [128, 32, 16], F32)
    creT = HP("creT", [128, 32, 16]); cimT = HP("cimT", [128, 32, 16])
    dcol = HP("dcol", [128, 32])
    dtt = small("dtt"); lre = small("lre"); aa = small("aa"); th = small("th"); mag = small("mag")
    cc = small("cc"); ss = small("ss"); u1 = small("u1"); u2 = small("u2"); Lre = small("Lre"); Lim = small("Lim")
    nre = small("nre"); den = small("den"); fre = small("fre"); fim = small("fim")
    iLre = small("iLre"); iLim = small("iLim"); halfpi = AR.alloc("halfpi", [128, 1], F32)
    PWr = AR.alloc("PWr", [128, 32, 8], F32); PWi = AR.alloc("PWi", [128, 32, 8], F32)
    NPr = AR.alloc("NPr", [128, 32, 8], F32); NPi = AR.alloc("NPi", [128, 32, 8], F32)
    PWsr = HP("PWsr", [128, 32, 8]); PWsi = HP("PWsi", [128, 32, 8])
    NPsr = HP("NPsr", [128, 32, 8]); NPsi = HP("NPsi", [128, 32, 8])
    bbr = HP("bbr", [128, 32, 16]); bbi = HP("bbi", [128, 32, 16])
    bsr = HP("bsr", [128, 32, 16]); bsi = HP("bsi", [128, 32, 16])

    def D1(fn):
        S.op('dve', fn, reads=['E0'], writes=['E0'])

    def A1(fn):
        S.op('act', fn, reads=['E0'], writes=['E0'])

    def TT(o, a, b, op):
        D1(lambda e: e.tensor_tensor(out=o, in0=a, in1=b, op=op))

    ecst = small("ecst"); xs_ = small("xs_"); x2_ = small("x2_")

    def Ppow(o, base, ex):
        S.op('pool', lambda e: e.tensor_tensor(out=o, in0=base, in1=ex, op=ALU.pow), reads=['E0'], writes=['E0'])

    def horner(o, coefs):
        D1(lambda e: e.tensor_scalar(out=o, in0=x2_[:], scalar1=coefs[0], scalar2=coefs[1], op0=ALU.mult, op1=ALU.add))
        for c_ in coefs[2:]:
            TT(o, o, x2_[:], ALU.mult)
            D1(lambda e, c_=c_: e.tensor_scalar(out=o, in0=o, scalar1=c_, scalar2=None, op0=ALU.add))
    D1(lambda e: e.memset(ecst[:], math.e))
    Ppow(dtt[:], ecst[:], logdt[:])
    D1(lambda e: e.tensor_scalar(out=lre[:], in0=lamre[:], scalar1=-1e-4, scalar2=None, op0=ALU.min))
    TT(aa[:], lre[:], dtt[:], ALU.mult)
    TT(th[:], lamim[:], dtt[:], ALU.mult)
    Ppow(mag[:], ecst[:], aa[:])
    D1(lambda e: e.tensor_scalar(out=u1[:], in0=aa[:], scalar1=8.0, scalar2=None, op0=ALU.mult))
    Ppow(rho[:], ecst[:], u1[:])
    D1(lambda e: e.tensor_scalar(out=xs_[:], in0=th[:], scalar1=1.0 / 16, scalar2=None, op0=ALU.mult))
    TT(x2_[:], xs_[:], xs_[:], ALU.mult)
    horner(ss[:], [1.0 / 362880, -1.0 / 5040, 1.0 / 120, -1.0 / 6, 1.0])
    TT(ss[:], ss[:], xs_[:], ALU.mult)
    horner(cc[:], [-1.0 / 3628800, 1.0 / 40320, -1.0 / 720, 1.0 / 24, -0.5, 1.0])
    for _ in range(4):
        TT(u1[:], cc[:], cc[:], ALU.mult)
        TT(u2[:], ss[:], ss[:], ALU.mult)
        D1(lambda e: e.scalar_tensor_tensor(out=ss[:], in0=cc[:], scalar=2.0, in1=ss[:], op0=ALU.mult, op1=ALU.mult))
        TT(cc[:], u1[:], u2[:], ALU.subtract)
    TT(Lre[:], mag[:], cc[:], ALU.mult)
    TT(Lim[:], mag[:], ss[:], ALU.mult)
    D1(lambda e: e.tensor_scalar(out=phi[:], in0=th[:], scalar1=8.0, scalar2=None, op0=ALU.mult))
    D1(lambda e: e.tensor_scalar(out=nre[:], in0=Lre[:], scalar1=-1.0, scalar2=None, op0=ALU.add))
    TT(den[:], lre[:], lre[:], ALU.mult)
    TT(u1[:], lamim[:], lamim[:], ALU.mult)
    TT(den[:], den[:], u1[:], ALU.add)
    D1(lambda e: e.reciprocal(out=den[:], in_=den[:]))
    TT(fre[:], nre[:], lre[:], ALU.mult)
    TT(u1[:], Lim[:], lamim[:], ALU.mult)
    TT(fre[:], fre[:], u1[:], ALU.add)
    TT(fre[:], fre[:], den[:], ALU.mult)
    TT(fim[:], Lim[:], lre[:], ALU.mult)
    TT(u1[:], nre[:], lamim[:], ALU.mult)
    TT(fim[:], fim[:], u1[:], ALU.subtract)
    TT(fim[:], fim[:], den[:], ALU.mult)
    TT(u1[:], mag[:], mag[:], ALU.mult)
    D1(lambda e: e.reciprocal(out=u1[:], in_=u1[:]))
    TT(iLre[:], Lre[:], u1[:], ALU.mult)
    D1(lambda e: e.scalar_tensor_tensor(out=iLim[:], in0=Lim[:], scalar=-1.0, in1=u1[:], op0=ALU.mult, op1=ALU.mult))

    def cmul(o_r, o_i, a_r, a_i, b_r, b_i, t1, t2):
        TT(t1, a_r, b_r, ALU.mult); TT(t2, a_i, b_i, ALU.mult); TT(o_r, t1, t2, ALU.subtract)
        TT(t1, a_r, b_i, ALU.mult); TT(t2, a_i, b_r, ALU.mult); TT(o_i, t1, t2, ALU.add)

    pwt = {tag: [small(f"pwt_{tag}{k}") for k in range(4)] for tag in ('P', 'N')}

    def cm_mults(tag, a_r, a_i, b_r, b_i):
        t = pwt[tag]
        rd = ['E0', ('pwr', tag), ('pwi', tag)]
        for k, (x_, y_) in enumerate(((a_r, b_r), (a_i, b_i), (a_r, b_i), (a_i, b_r))):
            S.op('dve', lambda e, k=k, x_=x_, y_=y_: e.tensor_tensor(out=t[k][:], in0=x_, in1=y_, op=ALU.mult),
                 reads=rd, writes=[('pwt', tag, k)])

    def cm_comb(tag, o_r, o_i):
        t = pwt[tag]
        S.op('dve', lambda e: e.tensor_tensor(out=o_r, in0=t[0][:], in1=t[1][:], op=ALU.subtract),
             reads=[('pwt', tag, 0), ('pwt', tag, 1)], writes=[('pwr', tag)])
        S.op('dve', lambda e: e.tensor_tensor(out=o_i, in0=t[2][:], in1=t[3][:], op=ALU.add),
             reads=[('pwt', tag, 2), ('pwt', tag, 3)], writes=[('pwi', tag)])
    chains = (('P', PWr, PWi, Lre, Lim), ('N', NPr, NPi, iLre, iLim))
    for (tag, Pr, Pi, Br, Bi) in chains:
        S.op('dve', lambda e, Pr=Pr: e.memset(Pr[:, :, 0:1], 1.0), reads=['E0'], writes=[('pwr', tag)])
        S.op('dve', lambda e, Pi=Pi: e.memset(Pi[:, :, 0:1], 0.0), reads=['E0'], writes=[('pwi', tag)])
    for j in range(1, 8):
        for (tag, Pr, Pi, Br, Bi) in chains:
            cm_mults(tag, Pr[:, :, j - 1], Pi[:, :, j - 1], Br[:], Bi[:])
        for (tag, Pr, Pi, Br, Bi) in chains:
            cm_comb(tag, Pr[:, :, j], Pi[:, :, j])
    PK = [('pwr', 'P'), ('pwi', 'P'), ('pwr', 'N'), ('pwi', 'N')]
    for (src, dst) in ((PWr, PWsr), (PWi, PWsi), (NPr, NPsr), (NPi, NPsi)):
        S.op('dve', lambda e, src=src, dst=dst: e.tensor_copy(out=dst[:, 0:16, :], in_=src[:, 0:16, :]), reads=['E0'] + PK, writes=['E0'])
        S.op('dve', lambda e, src=src, dst=dst: e.tensor_copy(out=dst[:, 16:32, :], in_=src[:, 16:32, ::-1]), reads=['E0'] + PK, writes=['E0'])
    t3a = AR.alloc("t3a", [128, 32, 16], F32)[:]; t3b = AR.alloc("t3b", [128, 32, 16], F32)[:]
    fb = lambda t: t[:].unsqueeze(2).to_broadcast([128, 32, 16])
    cmul(bbr[:], bbi[:], fb(fre), fb(fim), breT[:], bimT[:], t3a, t3b)
    TT(bsr[:], bbr[:], fb(rho), ALU.mult)
    TT(bsi[:], bbi[:], fb(rho), ALU.mult)
    del S.op
    for nm, tt_ in (("lamre", lamre), ("lamim", lamim), ("logdt", logdt), ("bre", breT), ("bim", bimT), ("cre", creT),
                    ("cim", cimT), ("dcol", dcol)):
        S.dma('sp', lambda e, nm=nm, tt_=tt_: e.dma_start(out=tt_[:], in_=din[nm]), writes=['E0'])
    n_head = 0
    for k_, op_ in enumerate(chain_ops):
        if op_[0] == 'act':
            n_head = k_ + 1
    for _ in range(n_head):
        S.op(*chain_ops.pop(0))
    iters = [(h, qc) for h in range(8) for qc in range(4)]
    NS = len(iters) * 9
    ob = 6

    def s_mm(s):
        it, kp = divmod(s, 9)
        h, qc = iters[it]
        kv = h // 4; hp = h // 2; h2 = h % 2
        qs = slice(qc * 512, (qc + 1) * 512)
        pr = s % 3
        for hf in range(2):
            kt = 2 * kp + hf
            S.op('pe', lambda e, kt=kt, hf=hf: e.matmul(PS[2 * pr + hf][:, :], lhsT=kTz[:, kv, h2, kt * 128:(kt + 1) * 128],
                                                      rhs=qT[:, hp, qs], start=True, stop=True),
                 reads=[], writes=[('ps', 2 * pr + hf)])

    def step(s):
        it, kp = divmod(s, 9)
        h, qc = iters[it]
        kv = h // 4; hp = h // 2; h2 = h % 2; ro = h2 * 64
        qs = slice(qc * 512, (qc + 1) * 512)
        pr = s % 3
        ot = osb[it % 2]
        ob = 6 + it % 2
        S.op('act', lambda e: e.activation(out=pT2[pr][:], in_=PSP[pr][:, :], func=AF.Exp, scale=0.125),
             reads=[('ps', 2 * pr), ('ps', 2 * pr + 1)], writes=[('pT', pr)])
        for hf in range(2):
            kt = 2 * kp + hf
            S.op('pe', lambda e, kt=kt, hf=hf: e.matmul(PS[ob][0:65, :], lhsT=Vaug[:, kt, kv, :], rhs=pT2[pr][:, hf * 512:(hf + 1) * 512],
                                                      start=(kt == 0), stop=(kt == 17)),
                 reads=[('pT', pr)], writes=[('ps', ob)])
        if kp == 1 and pending_fin:
            pending_fin[0][0]()
        if kp == 4 and pending_fin:
            pending_fin.pop()[1]()
        if kp == 8:
            S.op('dve', lambda e: e.tensor_copy(out=ot[0:65, :], in_=PS[ob][0:65, :]),
                 reads=[('ps', ob)], writes=[('osb', it % 2)])

            def fin_a():
                S.op('dve', lambda e: e.reciprocal(out=rsum[64:65, :], in_=ot[64:65, :]), reads=[('osb', it % 2)], writes=['rsum'])

            def fin():
                S.op('pe', lambda e: e.matmul(PS[ob][0:64, :], lhsT=ones_f[64:65, 0:64], rhs=rsum[64:65, :], start=True, stop=True),
                     reads=['rsum', 'ones_f'], writes=[('ps', ob)])
                S.op('dve', lambda e: e.tensor_tensor(out=tmpD[0:64, :], in0=ot[0:64, :], in1=PS[ob][0:64, :], op=ALU.mult),
                     reads=[('ps', ob), ('osb', it % 2)], writes=['tmpD'])
                S.op('dve', lambda e: e.tensor_tensor(out=ygT[ro:ro + 64, hp, qs], in0=tmpD[0:64, :], in1=sgT[0:64, h, qs], op=ALU.mult),
                     reads=['tmpD'] + [('sgT', h, tb) for tb in range(4)], writes=[('ygT', h, qc)])
            pending_fin.append((fin_a, fin))
    pending_fin = []
    s_mm(0)
    s_mm(1)
    for s in range(NS):
        if s + 2 < NS:
            s_mm(s + 2)
        step(s)
        if s >= 12 and 2 <= s % 9 <= 7:
            for _ in range(2):
                if chain_ops:
                    S.op(*chain_ops.pop(0))
    while chain_ops:
        S.op(*chain_ops.pop(0))
    while pending_fin:
        fa, fb = pending_fin.pop()
        fa(); fb()
    dump("ygT", ygT[:], [128, 4, NLAT], BF16, reads=[('ygT', h, qc) for h in range(8) for qc in range(4)])
    S.flush()
    AR.release(m_mid2)
    if stop == 'D':
        return nc, din, consts

    AR.hi = HI_WTM
    ygsT = nc.alloc_sbuf_tensor_at("ygsT_alias", [128, 4, NLAT], BF16, offset=AR.offs["utm"])
    m_E = AR.mark()
    rho_k = AR.alloc("rho", [128, 32], F32); phi_k = AR.alloc("phi", [128, 32], F32)
    pos = AR.alloc("pos", [128, 2, NCH], F32)
    CtRe = AR.alloc("CtRe", [128, 32, 128], BF16); nCtIm = AR.alloc("nCtIm", [128, 32, 128], BF16)
    GsTr = AR.alloc("GsTr", [128, 32, 128], BF16); GsTi = AR.alloc("GsTi", [128, 32, 128], BF16)
    WT = AR.alloc("WT", [128, 32, 2, 128], BF16)
    m_E0 = AR.mark()
    maskF = AR.alloc("maskF", [128, 128], F32); maskB = AR.alloc("maskB", [128, 128], F32)
    for nm, tt_ in (("maskF", maskF), ("maskB", maskB), ("pos", pos)):
        S.dma('sp', lambda e, nm=nm, tt_=tt_: e.dma_start(out=tt_[:], in_=din[nm]), writes=['E0'])
    S.op('dve', lambda e: e.tensor_copy(out=rho_k[:], in_=rho[:]), reads=['E0'], writes=['E0'])
    S.op('dve', lambda e: e.tensor_copy(out=phi_k[:], in_=phi[:]), reads=['E0'], writes=['E0'])
    m_E1 = AR.mark()
    Gsr = AR.alloc("Gsr", [128, 32, 128], BF16); Gsi = AR.alloc("Gsi", [128, 32, 128], BF16)
    Gpr = AR.alloc("Gpr", [128, 32, 128], BF16); Gpi = AR.alloc("Gpi", [128, 32, 128], BF16)
    T1 = AR.alloc("T1", [128, 2048], F32); T2 = AR.alloc("T2", [128, 2048], F32)
    t4a = T1[:].rearrange("p (q s h) -> p q s h", s=8, h=16); t4b = T2[:].rearrange("p (q s h) -> p q s h", s=8, h=16)
    P1_ = AR.alloc("P1", [128, 1024], F32); P2_ = AR.alloc("P2", [128, 1024], F32)
    p4a = P1_[:].rearrange("p (q s h) -> p q s h", s=8, h=16); p4b = P2_[:].rearrange("p (q s h) -> p q s h", s=8, h=16)

    def TTp(o, a, b, op):
        S.op('pool', lambda e: e.tensor_tensor(out=o, in0=a, in1=b, op=op), reads=['Ct'], writes=['Ct'])
    for qq in range(4):
        qs8 = slice(qq * 8, (qq + 1) * 8)
        v4p = lambda t, qs8=qs8: t[:, qs8, :].rearrange("p q (s h) -> p q s h", h=16)
        pb4p = lambda t, qs8=qs8: t[:, qs8, :].unsqueeze(3).to_broadcast([128, 8, 8, 16])
        hb4p = lambda t, qs8=qs8: t[:, qs8, :].unsqueeze(2).to_broadcast([128, 8, 8, 16])
        TTp(p4a, hb4p(creT), pb4p(PWsr), ALU.mult); TTp(p4b, hb4p(cimT), pb4p(PWsi), ALU.mult)
        TTp(v4p(CtRe), p4a, p4b, ALU.subtract)
        TTp(p4a, hb4p(creT), pb4p(PWsi), ALU.mult); TTp(p4b, hb4p(cimT), pb4p(PWsr), ALU.mult)
        S.op('pool', lambda e: e.tensor_scalar(out=p4a, in0=p4a, scalar1=-1.0, scalar2=1.0, op0=ALU.mult, op1=ALU.mult),
             reads=['Ct'], writes=['Ct'])
        TTp(v4p(nCtIm), p4a, p4b, ALU.subtract)
    for hq in range(2):
        qs_ = slice(hq * 16, (hq + 1) * 16)
        v4 = lambda t: t[:, qs_, :].rearrange("p q (s h) -> p q s h", h=16)
        pb4 = lambda t: t[:, qs_, :].unsqueeze(3).to_broadcast([128, 16, 8, 16])
        hb4 = lambda t: t[:, qs_, :].unsqueeze(2).to_broadcast([128, 16, 8, 16])
        cmul(v4(Gsr), v4(Gsi), pb4(NPsr), pb4(NPsi), hb4(bbr), hb4(bbi), t4a, t4b)
        cmul(v4(Gpr), v4(Gpi), pb4(NPsr), pb4(NPsi), hb4(bsr), hb4(bsi), t4a, t4b)
    for part, (Gp, GT) in enumerate(((Gpr, GsTr), (Gpi, GsTi))):
        for q0 in range(0, 32, 8):
            bk = (part * 4 + q0 // 8) % 4
            pbv = PS[bk][:].bitcast(BF16)
            for qq in range(8):
                S.op('pe', lambda e, Gp=Gp, q0=q0, qq=qq, pbv=pbv: e.transpose(out=pbv[:, qq * 128:(qq + 1) * 128], in_=Gp[:, q0 + qq, :],
                                                                           identity=ident_b[:]), reads=['E0'], writes=[('ps', bk)])
            S.op('act', lambda e, GT=GT, q0=q0, pbv=pbv: e.activation(out=GT[:, q0:q0 + 8, :], in_=pbv.rearrange("p (a b) -> p a b", b=128),
                                                                    func=AF.Copy), reads=[('ps', bk)], writes=['GsT'])
    tmpW = T1[:, 0:512].rearrange("p (a b) -> p a b", b=128)
    for q in range(32):
        dr = q // 16
        mk = maskF if dr == 0 else maskB
        for g2 in range(2):
            bk = 4 + (q % 2) * 2 + g2
            rs = slice(g2 * 64, (g2 + 1) * 64)
            S.op('pe', lambda e, q=q, rs=rs, bk=bk: e.matmul(PS[bk][:, 0:128], lhsT=Gsr[rs, q, :], rhs=CtRe[rs, q, :], start=True, stop=False),
                 reads=['E0', 'Ct'], writes=[('ps', bk)])
            S.op('pe', lambda e, q=q, rs=rs, bk=bk: e.matmul(PS[bk][:, 0:128], lhsT=Gsi[rs, q, :], rhs=nCtIm[rs, q, :], start=False, stop=True),
                 reads=['E0', 'Ct'], writes=[('ps', bk)])
            if dr == 1:
                S.op('dve', lambda e, q=q, g2=g2, bk=bk, mk=mk: e.tensor_tensor(out=WT[:, q, g2, :], in0=PS[bk][:, 0:128], in1=mk[:], op=ALU.mult),
                     reads=[('ps', bk), 'E0'], writes=['E0'])
            else:
                g = 2 * (q % 16) + g2
                S.op('dve', lambda e, g2=g2, bk=bk, mk=mk: e.tensor_tensor(out=tmpW[:, g2, :], in0=PS[bk][:, 0:128], in1=mk[:], op=ALU.mult),
                     reads=[('ps', bk), 'E0'], writes=['E0'])
                D1(lambda e, q=q, g2=g2, g=g: e.scalar_tensor_tensor(out=WT[:, q, g2, :], in0=ident_f[:], scalar=dcol[:, g:g + 1],
                                                                   in1=tmpW[:, g2, :], op0=ALU.mult, op1=ALU.add))
    dump("WT", WT[:], [128, 32, 2, 128], BF16, reads=['E0'])
    dump("GsTr", GsTr[:], [128, 32, 128], BF16, reads=['GsT', 'E0'])
    dump("CtRe", CtRe[:], [128, 32, 128], BF16, reads=['E0'])
    S.flush()
    AR.release(m_E0)
    AR.hi = 229344
    Xo = AR.alloc("Xo", [128, 2, 8, 512], BF16)
    m_Xo = AR.mark()
    U = AR.alloc("U", [128, 32, NCH], BF16)
    m_E1 = AR.mark()
    sel = AR.alloc("sel", [128, 8, 8, 128], BF16)
    X2 = AR.alloc("X2", [128, 3, 32, 128], BF16)
    S.dma('sp', lambda e: e.dma_start(out=sel[:], in_=din["sel"]), writes=['sel'])
    for ct in range(3):
        nj = 8 if ct < 2 else 2
        for hf in range(2):
            for s4 in range(4):
                s_ = hf * 4 + s4
                for j in range(nj):
                    S.op('pe', lambda e, ct=ct, s4=s4, s_=s_, j=j, nj=nj: e.matmul(PS[s4][:, :], lhsT=sel[:, j, s_, :], rhs=utm[:, 8 * ct + j, :],
                                                                                 start=(j == 0), stop=(j == nj - 1)),
                         reads=['sel'], writes=[('ps', s4)])
                S.op('act' if s4 % 2 == 0 else 'dve',
                     (lambda e, ct=ct, s4=s4, s_=s_: e.activation(out=X2[:, ct, :, s_ * 16:(s_ + 1) * 16],
                                                                  in_=PS[s4][:, :].rearrange("p (g h) -> p g h", h=16), func=AF.Copy))
                     if s4 % 2 == 0 else
                     (lambda e, ct=ct, s4=s4, s_=s_: e.tensor_copy(out=X2[:, ct, :, s_ * 16:(s_ + 1) * 16],
                                                                   in_=PS[s4][:, :].rearrange("p (g h) -> p g h", h=16))),
                     reads=[('ps', s4)], writes=[('X2', ct, s_)])
    for g0 in range(0, 32, 3):
        ng = min(3, 32 - g0)
        bk = 4 + (g0 // 3) % 4
        pbv = PS[bk][:].bitcast(BF16)
        for gi in range(ng):
            g = g0 + gi
            for ct in range(3):
                nr = 128 if ct < 2 else 32
                S.op('pe', lambda e, g=g, gi=gi, ct=ct, nr=nr, pbv=pbv: e.transpose(
                    out=pbv[:, gi * NCH + ct * 128:gi * NCH + ct * 128 + nr], in_=X2[0:nr, ct, g, :], identity=ident_b[0:nr, 0:nr]),
                    reads=[('X2', ct, s_) for s_ in range(8)], writes=[('ps', bk)])
        S.op('act' if (g0 // 3) % 2 == 0 else 'dve',
             (lambda e, g0=g0, ng=ng, pbv=pbv: e.activation(out=U[:, g0:g0 + ng, :], in_=pbv[:, 0:ng * NCH].rearrange("p (a b) -> p a b", b=NCH),
                                                            func=AF.Copy)) if (g0 // 3) % 2 == 0 else
             (lambda e, g0=g0, ng=ng, pbv=pbv: e.tensor_copy(out=U[:, g0:g0 + ng, :], in_=pbv[:, 0:ng * NCH].rearrange("p (a b) -> p a b", b=NCH))),
             reads=[('ps', bk)], writes=['U'])
    dump("U", U[:], [128, 32, NCH], BF16, reads=['U'])
    S.flush()
    AR.release(m_E1)
    if stop == 'E1':
        return nc, din, consts
    Ere_ = [AR.alloc(f"Ere{i}", [128, 4, NCH], F32) for i in range(2)]
    Eim_ = [AR.alloc(f"Eim{i}", [128, 4, NCH], F32) for i in range(2)]
    ang = AR.alloc("ang", [128, 4, NCH], F32); kfi = AR.alloc("kfi", [128, 4, NCH], I32)
    Rre = AR.alloc("Rre", [128, 2, 4, NCH], BF16); Rim = AR.alloc("Rim", [128, 2, 4, NCH], BF16)
    Xr = AR.alloc("Xr", [128, NCH], F32); Xi = AR.alloc("Xi", [128, NCH], F32)
    Wa = AR.alloc("Wa", [128, NCH], F32); Wb = AR.alloc("Wb", [128, NCH], F32)
    Pa = AR.alloc("Pa", [128, NCH], F32); Pb = AR.alloc("Pb", [128, NCH], F32)
    Zr_ = [AR.alloc(f"Zr{i}", [128, NCH + 2], F32) for i in range(2)]
    Zi_ = [AR.alloc(f"Zi{i}", [128, NCH + 2], F32) for i in range(2)]
    Wa2 = nc.alloc_sbuf_tensor_at("Wa2", [128, 2, NCH], F32, offset=AR.offs["utm"] + 1024)
    Wb2 = nc.alloc_sbuf_tensor_at("Wb2", [128, 2, NCH], F32, offset=AR.offs["utm"] + 1024 + 2304)
    Ysb_ = [nc.alloc_sbuf_tensor_at(f"Ysb{i}", [128, 256], BF16, offset=AR.offs["utm"] + 512 * i) for i in range(2)]
    C1 = 6.28125
    C2 = 2 * math.pi - C1
    for zz in Zr_ + Zi_:
        S.op('dve', lambda e, zz=zz: e.memset(zz[:], 0.0), writes=[('Z', 0), ('Z', 1)])
    gcount = [0]

    def build_tables(qt, dr):
        q0 = dr * 16 + qt * 4
        Ere = Ere_[dr]; Eim = Eim_[dr]
        EK = ('E', dr)
        S.op('pool', lambda e: e.tensor_tensor(out=ang[:], in0=pos[:, dr, :].unsqueeze(1).to_broadcast([128, 4, NCH]),
                                               in1=phi_k[:, q0:q0 + 4].unsqueeze(2).to_broadcast([128, 4, NCH]), op=ALU.mult),
             reads=[], writes=['ang'])
        S.op('dve', lambda e: e.tensor_scalar(out=kfi[:], in0=ang[:], scalar1=1.0 / (2 * math.pi), scalar2=None, op0=ALU.mult),
             reads=['ang'], writes=['kfi'])
        S.op('dve', lambda e: e.scalar_tensor_tensor(out=ang[:], in0=kfi[:], scalar=-C1, in1=ang[:], op0=ALU.mult, op1=ALU.add),
             reads=['kfi', 'ang'], writes=['ang'])
        S.op('dve', lambda e: e.scalar_tensor_tensor(out=ang[:], in0=kfi[:], scalar=-C2, in1=ang[:], op0=ALU.mult, op1=ALU.add),
             reads=['kfi', 'ang'], writes=['ang'])
        S.op('dve', lambda e: e.tensor_scalar(out=ang[:], in0=ang[:], scalar1=3.141592, scalar2=-3.141592, op0=ALU.min, op1=ALU.max),
             reads=['ang'], writes=['ang'])
        S.op('act', lambda e: e.activation(out=Eim[:], in_=ang[:], func=AF.Sin, scale=-1.0), reads=['ang'], writes=[EK])
        S.op('dve', lambda e: e.scalar_tensor_tensor(out=ang[:], in0=ang[:], scalar=-1.0, in1=ang[:], op0=ALU.mult, op1=ALU.max),
             reads=['ang'], writes=['ang'])
        S.op('act', lambda e: e.activation(out=Ere[:], in_=ang[:], func=AF.Sin, scale=-1.0, bias=halfpi2[:]), reads=['ang'], writes=[EK])


    def do_gl(qt, dr, gl):
        q0 = dr * 16 + qt * 4
        Ere = Ere_[dr]; Eim = Eim_[dr]
        EK = ('E', dr)
        q = q0 + gl
        gp = qt * 4 + gl
        par = gcount[0] % 2
        gcount[0] += 1
        Zr = Zr_[par]; Zi = Zi_[par]
        ZK = ('Z', par)
        bre_, bim_ = par * 2, par * 2 + 1
        for g2 in range(2):
            g = 2 * gp + g2
            S.op('pe', lambda e, g=g, g2=g2: e.matmul(PS[bre_][g2 * 64:(g2 + 1) * 64, 0:NCH], lhsT=GsTr[:, q, g2 * 64:(g2 + 1) * 64],
                                                    rhs=U[:, g, :], start=True, stop=True), reads=[], writes=[('ps', bre_)])
            S.op('pe', lambda e, g=g, g2=g2: e.matmul(PS[bim_][g2 * 64:(g2 + 1) * 64, 0:NCH], lhsT=GsTi[:, q, g2 * 64:(g2 + 1) * 64],
                                                    rhs=U[:, g, :], start=True, stop=True), reads=[], writes=[('ps', bim_)])
        Er = Ere[:, gl, :]; Ei = Eim[:, gl, :]
        Vr = PS[bre_][:, 0:NCH]; Vi = PS[bim_][:, 0:NCH]

        def V(fn, rd, wr):
            S.op('dve', fn, reads=rd, writes=wr)
        Vp = PSP[par][:, :].rearrange("p (b c) -> p b c", b=2)[:, :, 0:NCH]
        Er2 = Er.unsqueeze(1).to_broadcast([128, 2, NCH]); Ei2 = Ei.unsqueeze(1).to_broadcast([128, 2, NCH])
        V(lambda e: e.tensor_tensor(out=Wa2[:], in0=Vp, in1=Er2, op=ALU.mult), [('ps', bre_), ('ps', bim_), EK], ['Wa'])
        V(lambda e: e.tensor_tensor(out=Wb2[:], in0=Vp, in1=Ei2, op=ALU.mult), [('ps', bre_), ('ps', bim_), EK], ['Wb'])
        V(lambda e: e.tensor_tensor(out=Xr[:], in0=Wa2[:, 0, :], in1=Wb2[:, 1, :], op=ALU.subtract), ['Wa', 'Wb'], ['Xr'])
        V(lambda e: e.tensor_tensor(out=Xi[:], in0=Wb2[:, 0, :], in1=Wa2[:, 1, :], op=ALU.add), ['Wa', 'Wb'], ['Xi'])
        rb = rho_k[:, q:q + 1]
        for (Xx, Zz, nmx) in ((Xr, Zr, 'Xr'), (Xi, Zi, 'Xi')):
            if dr == 0:
                V(lambda e, Xx=Xx, Zz=Zz: e.tensor_tensor_scan(out=Zz[:, 257:289], data0=rb.to_broadcast([128, 32]), data1=Xx[:, 256:288],
                                                               initial=0.0, op0=ALU.mult, op1=ALU.add), [nmx], [ZK])
                V(lambda e, Zz=Zz: e.tensor_copy(out=Zz[:, 0:1], in_=Zz[:, 288:289]), [ZK], [ZK])
                V(lambda e, Xx=Xx, Zz=Zz: e.tensor_tensor_scan(out=Zz[:, 1:257], data0=rb.to_broadcast([128, 256]), data1=Xx[:, 0:256],
                                                               initial=Zz[:, 288:289], op0=ALU.mult, op1=ALU.add), [nmx, ZK], [ZK])
            else:
                V(lambda e, Zz=Zz: e.memset(Zz[:, 288:290], 0.0), [], [ZK])
                V(lambda e, Xx=Xx, Zz=Zz: e.tensor_tensor_scan(out=Zz[:, 0:288][:, ::-1], data0=rb.to_broadcast([128, NCH]), data1=Xx[:, ::-1],
                                                               initial=0.0, op0=ALU.mult, op1=ALU.add), [nmx, ZK], [ZK])
        zo = 0 if dr == 0 else 1
        Zpr = Zr[:, zo:zo + NCH]; Zpi = Zi[:, zo:zo + NCH]
        RK = ('Rw', dr, gl)

        def P(fn, rd, wr):
            S.op('pool', fn, reads=rd, writes=wr)
        P(lambda e: e.tensor_tensor(out=Pa[:], in0=Er, in1=Zpr, op=ALU.mult), [EK, ZK], ['Pa'])
        P(lambda e: e.tensor_tensor(out=Pb[:], in0=Ei, in1=Zpi, op=ALU.mult), [EK, ZK], ['Pb'])
        P(lambda e: e.tensor_tensor(out=Rre[:, dr, gl, :], in0=Pa[:], in1=Pb[:], op=ALU.add), ['Pa', 'Pb'], [RK])
        P(lambda e: e.tensor_tensor(out=Pa[:], in0=Er, in1=Zpi, op=ALU.mult), [EK, ZK], ['Pa'])
        P(lambda e: e.tensor_tensor(out=Pb[:], in0=Ei, in1=Zpr, op=ALU.mult), [EK, ZK], ['Pb'])
        P(lambda e: e.tensor_tensor(out=Rim[:, dr, gl, :], in0=Pa[:], in1=Pb[:], op=ALU.subtract), ['Pa', 'Pb'], [RK])
        if dr == 0:
            P(lambda e: e.memset(Rre[:, 0, gl, 256:257], 0.0), [], [RK])
            P(lambda e: e.memset(Rim[:, 0, gl, 256:257], 0.0), [], [RK])

    def y_group(qt, gl):
        if True:
            gp = qt * 4 + gl
            for g2 in range(2):
                g = 2 * gp + g2
                yb = 4 + g % 2
                Ysb = Ysb_[g % 2]; YK = ('Ysb', g % 2)
                rs = slice(g2 * 64, (g2 + 1) * 64)
                k = 0
                for dr in range(2):
                    q = dr * 16 + gp
                    for (lt, rh) in ((CtRe[rs, q, :], Rre[rs, dr, gl, :]), (nCtIm[rs, q, :], Rim[rs, dr, gl, :]), (WT[:, q, g2, :], U[:, g, :])):
                        S.op('pe', lambda e, lt=lt, rh=rh, k=k, yb=yb: e.matmul(PS[yb][:, 0:NCH], lhsT=lt, rhs=rh, start=(k == 0), stop=(k == 5)),
                             reads=[('Rw', dr, gl)], writes=[('ps', yb)])
                        k += 1
                S.op('act', lambda e, yb=yb, Ysb=Ysb: e.activation(out=Ysb[:], in_=PS[yb][:, 0:256], func=AF.Copy), reads=[('ps', yb)], writes=[YK])
                tb_ = 6 + g % 2
                ptv = PS[tb_][:].bitcast(BF16)
                for ct in range(2):
                    S.op('pe', lambda e, ct=ct, ptv=ptv, Ysb=Ysb: e.transpose(out=ptv[:, ct * 128:(ct + 1) * 128], in_=Ysb[:, ct * 128:(ct + 1) * 128],
                                                                     identity=ident_b[:]), reads=[YK], writes=[('ps', tb_)])
                S.op('act', lambda e, g=g, ptv=ptv: e.activation(out=Xo[:, :, :, g * 16:(g + 1) * 16],
                                                                 in_=ptv[:, 0:256].rearrange("p (c t h) -> p c t h", c=2, h=16), func=AF.Copy),
                     reads=[('ps', tb_)], writes=[('Xo', g)])


    halfpi2 = AR.alloc("halfpi2", [128, 1], F32)
    S.op('dve', lambda e: e.memset(halfpi2[:], math.pi / 2), writes=[('E', 0), ('E', 1)])
    seq_ = [(qt, dr) for qt in range(4) for dr in range(2)]
    build_tables(*seq_[0])
    for k_, (qt, dr) in enumerate(seq_):
        for gl in range(4):
            if dr == 0 and qt > 0:
                y_group(qt - 1, gl)
            do_gl(qt, dr, gl)
            if gl == 1 and k_ + 1 < len(seq_):
                build_tables(*seq_[k_ + 1])
    for gl in range(4):
        y_group(3, gl)
    dump("Xo", Xo[:], [128, 2, 8, 512], BF16, reads=[('Xo', g) for g in range(32)])
    S.flush()
    AR.release(m_Xo)
    if stop == 'E2':
        return nc, din, consts
    selT = AR.alloc("selT", [128, 8, 512], BF16)
    zT = AR.alloc("zT", [128, 4, NLAT], BF16)
    wglu = AR.alloc("wglu", [128, 4, 512], BF16)
    wgb = AR.alloc("wgb", [128, 8, 512], BF16)
    bglu = AR.alloc("bglu", [128, 4], F32)
    sgl4 = [AR.alloc(f"sgl{i}", [128, 512], F32) for i in range(4)]
    sgb4 = [AR.alloc(f"sgb{i}", [128, 512], F32) for i in range(4)]
    tE = AR.alloc("tE", [128, 512], F32)
    S.dma('sp', lambda e: e.dma_start(out=selT[:], in_=din["selT"]), writes=['selT'])
    S.dma('sp', lambda e: e.dma_start(out=bglu[:], in_=din["b_gluT"]), writes=['bglu'])
    for ci in range(4):
        S.dma('pool', lambda e, ci=ci: e.dma_start(out=wglu[:, ci, :], in_=din["w_glu"][ci * 128:(ci + 1) * 128, :]), writes=['wglu'])
    for kt in range(8):
        S.dma('pool', lambda e, kt=kt: e.dma_start(out=wgb[:, kt, :], in_=din["w_in"][kt * 128:(kt + 1) * 128, 1792:2304]), writes=['wgb'])

    assert AR.offs["nCtIm"] == AR.offs["CtRe"] + 8192 and AR.offs["GsTi"] == AR.offs["GsTr"] + 8192
    wma = nc.alloc_sbuf_tensor_at("wma_pf", [128, 8, 1024], BF16, offset=AR.offs["CtRe"])
    wmb = nc.alloc_sbuf_tensor_at("wmb_pf", [128, 8, 1024], BF16, offset=AR.offs["GsTr"])
    wbra = nc.alloc_sbuf_tensor_at("wbra_pf", [128, 4, 1024], BF16, offset=AR.offs["WT"])
    wbrb = nc.alloc_sbuf_tensor_at("wbrb_pf", [128, 4, 1024], BF16, offset=AR.offs["WT"] + 8192)
    w_slot_end = AR.offs["WT"] + 16384
    for a in range(4):
        S.dma('pool', lambda e, a=a: e.dma_start(out=wbra[:, a, :], in_=din["w_bra"][a * 128:(a + 1) * 128, :]), writes=['wbra'])
        S.dma('pool', lambda e, a=a: e.dma_start(out=wbrb[:, a, :], in_=din["w_brb"][a * 128:(a + 1) * 128, :]), writes=['wbrb'])
    for kt in range(8):
        S.dma('pool', lambda e, kt=kt: e.dma_start(out=wma[:, kt, :], in_=din["w_in"][kt * 128:(kt + 1) * 128, 2304:3328]), writes=['wma'])
        S.dma('pool', lambda e, kt=kt: e.dma_start(out=wmb[:, kt, :], in_=din["w_in"][kt * 128:(kt + 1) * 128, 3328:4352]), writes=['wmb'])

    def e3(tb):
        ts_ = slice(tb * 512, (tb + 1) * 512)
        rows = slice((tb % 2) * 64, (tb % 2) * 64 + 64)
        ct = tb // 2
        for ci in range(4):
            b = ci % 2
            for s_ in range(8):
                S.op('pe', lambda e, ci=ci, s_=s_, b=b: e.matmul(PS[b][:, :], lhsT=Xo[rows, ct, s_, ci * 128:(ci + 1) * 128], rhs=selT[rows, s_, :],
                                                                 start=(s_ == 0), stop=(s_ == 7)), reads=['selT'], writes=[('ps', b)])
            S.op('act', lambda e, ci=ci, b=b: e.activation(out=zT[:, ci, ts_], in_=PS[b][:, :], func=AF.Gelu_apprx_tanh),
                 reads=[('ps', b)], writes=[('zT', ci)])
        for co in range(4):
            b1 = 2 + co % 2
            for ci in range(4):
                S.op('pe', lambda e, ci=ci, co=co, b1=b1: e.matmul(PS[b1][:, :], lhsT=wglu[:, ci, co * 128:(co + 1) * 128], rhs=zT[:, ci, ts_],
                                                                   start=(ci == 0), stop=(ci == 3)), reads=['wglu', ('zT', ci)], writes=[('ps', b1)])
            S.op('act', lambda e, co=co, b1=b1: e.activation(out=sgl4[co][:], in_=PS[b1][:, :], func=AF.Sigmoid, bias=bglu[:, co:co + 1]),
                 reads=[('ps', b1), 'bglu'], writes=[('sgl', co)])
        for co in range(4):
            b2 = 4 + co % 2
            for kt in range(8):
                S.op('pe', lambda e, kt=kt, co=co, b2=b2: e.matmul(PS[b2][:, :], lhsT=wgb[:, kt, co * 128:(co + 1) * 128], rhs=xnT[:, kt, ts_],
                                                                   start=(kt == 0), stop=(kt == 7)), reads=['wgb'], writes=[('ps', b2)])
            S.op('act', lambda e, co=co, b2=b2: e.activation(out=sgb4[co][:], in_=PS[b2][:, :], func=AF.Silu), reads=[('ps', b2)], writes=[('sgb', co)])
        for co in range(4):
            S.op('dve', lambda e, co=co: e.tensor_tensor(out=tE[:], in0=zT[:, co, ts_], in1=sgl4[co][:], op=ALU.mult),
                 reads=[('zT', co), ('sgl', co)], writes=['tE'])
            S.op('dve', lambda e, co=co: e.tensor_tensor(out=ygsT[:, co, ts_], in0=tE[:], in1=sgb4[co][:], op=ALU.mult),
                 reads=['tE', ('sgb', co)], writes=[('ygsT', co, tb)])
    for tb in range(4):
        e3(tb)
    dump("ygsT", ygsT[:], [128, 4, NLAT], BF16, reads=[('ygsT', co, tb) for co in range(4) for tb in range(4)])
    S.flush()
    AR.release(m_E)
    if stop == 'E3':
        return nc, din, consts

    AR.cur = max(AR.cur, w_slot_end)
    mixT = AR.alloc("mixT", [128, 8, NLAT], BF16)
    m_F = AR.mark()
    sa = AR.alloc("sa", [128, 512], F32); sb_ = AR.alloc("sb_", [128, 512], F32)
    f1 = AR.alloc("f1", [128, 512], F32); f2 = AR.alloc("f2", [128, 512], F32)

    def fstep(tb, j, n):
        ts_ = slice(tb * 512, (tb + 1) * 512); js = slice(j * 128, (j + 1) * 128)
        b0 = (n % 2) * 4
        for a in range(4):
            S.op('pe', lambda e, a=a: e.matmul(PS[b0][:, :], lhsT=wbra[:, a, js], rhs=ygT[:, a, ts_], start=(a == 0), stop=(a == 3)),
                 reads=['wbra'], writes=[('ps', b0)])
        for a in range(4):
            S.op('pe', lambda e, a=a: e.matmul(PS[b0 + 1][:, :], lhsT=wbrb[:, a, js], rhs=ygsT[:, a, ts_], start=(a == 0), stop=(a == 3)),
                 reads=['wbrb'], writes=[('ps', b0 + 1)])
        for kt in range(8):
            S.op('pe', lambda e, kt=kt: e.matmul(PS[b0 + 2][:, :], lhsT=wma[:, kt, js], rhs=xnT[:, kt, ts_], start=(kt == 0), stop=(kt == 7)),
                 reads=['wma'], writes=[('ps', b0 + 2)])
        for kt in range(8):
            S.op('pe', lambda e, kt=kt: e.matmul(PS[b0 + 3][:, :], lhsT=wmb[:, kt, js], rhs=xnT[:, kt, ts_], start=(kt == 0), stop=(kt == 7)),
                 reads=['wmb'], writes=[('ps', b0 + 3)])
        S.op('act', lambda e: e.activation(out=sa[:], in_=PS[b0 + 2][:, :], func=AF.Sigmoid), reads=[('ps', b0 + 2)], writes=['sa'])
        S.op('act', lambda e: e.activation(out=sb_[:], in_=PS[b0 + 3][:, :], func=AF.Sigmoid), reads=[('ps', b0 + 3)], writes=['sb_'])
        S.op('dve', lambda e: e.tensor_tensor(out=f1[:], in0=PS[b0][:, :], in1=sa[:], op=ALU.mult), reads=[('ps', b0), 'sa'], writes=['f1'])
        S.op('dve', lambda e: e.tensor_tensor(out=f2[:], in0=PS[b0 + 1][:, :], in1=sb_[:], op=ALU.mult), reads=[('ps', b0 + 1), 'sb_'], writes=['f2'])
        S.op('pool', lambda e: e.tensor_tensor(out=mixT[:, j, ts_], in0=f1[:], in1=f2[:], op=ALU.add), reads=['f1', 'f2'], writes=[('mixT', j, tb)])
    woutg = AR.alloc("woutg", [128, 8, 1024], BF16)
    wo = [AR.alloc(f"wo{i}", [128, 1024], F32) for i in range(2)]
    fg = AR.alloc("fg", [128, 1024], F32)
    xg = [AR.alloc(f"xg{i}", [128, 1024], F32) for i in range(3)]
    junkG = AR.alloc("junkG", [128, 1024], BF16)
    ssG = AR.alloc("ssG", [128, 16], F32)
    neghalf = AR.alloc("neghalf", [128, 1], F32)
    S.op('pool', lambda e: e.memset(neghalf[:], -0.5), writes=['neghalf'])
    S.dma('sp', lambda e: e.dma_start(out=fg[:], in_=din["fg_bc"]), writes=['fg'])
    for kt in range(8):
        w_ = wo[kt % 2]
        S.dma('sp', lambda e, kt=kt, w_=w_: e.dma_start(out=w_[:], in_=din["w_out"][kt * 128:(kt + 1) * 128, :]), writes=[('wo', kt % 2)])
        S.op('dve', lambda e, kt=kt, w_=w_: e.tensor_tensor(out=woutg[:, kt, :], in0=w_[:], in1=gate_bc[:], op=ALU.mult),
             reads=[('wo', kt % 2)], writes=[('woutg', kt)])

    def g_a(i, nb):
        xt_ = xg[i % 3]
        tb = i // 4
        S.dma('sp', lambda e: e.dma_start(out=xt_[:], in_=x_d[i * 128:(i + 1) * 128, :]), writes=[('xg', i % 3)])
        for c2 in range(2):
            b = nb + c2
            for kt in range(8):
                S.op('pe', lambda e, kt=kt, b=b, c2=c2: e.matmul(PS[b][:, :], lhsT=mixT[:, kt, i * 128:(i + 1) * 128], rhs=woutg[:, kt, c2 * 512:(c2 + 1) * 512],
                                                                start=(kt == 0), stop=(kt == 7)), reads=[('woutg', kt), ('mixT', kt, tb)], writes=[('ps', b)])
            S.op('dve', lambda e, b=b, c2=c2: e.tensor_tensor(out=xt_[:, c2 * 512:(c2 + 1) * 512], in0=PS[b][:, :], in1=xt_[:, c2 * 512:(c2 + 1) * 512], op=ALU.add),
                 reads=[('ps', b), ('xg', i % 3)], writes=[('xg', i % 3)])
        S.op('act', lambda e: e.activation(out=junkG[:], in_=xt_[:], func=AF.Square, accum_out=ssG[:, i:i + 1]),
             reads=[('xg', i % 3)], writes=['junkG', ('ssG', i)])

    def g_b(i):
        S.op('dve', lambda e: e.tensor_scalar(out=ssG[:, i:i + 1], in0=ssG[:, i:i + 1], scalar1=1.0 / D, scalar2=EPS, op0=ALU.mult, op1=ALU.add),
             reads=[('ssG', i)], writes=[('ssG', i)])
        S.op('pool', lambda e: e.tensor_tensor(out=ssG[:, i:i + 1], in0=ssG[:, i:i + 1], in1=neghalf[:, 0:1], op=ALU.pow),
             reads=[('ssG', i), 'neghalf'], writes=[('ssG', i)])

    def g_c(i):
        xt_ = xg[i % 3]
        S.op('dve', lambda e: e.scalar_tensor_tensor(out=xt_[:], in0=xt_[:], scalar=ssG[:, i:i + 1], in1=fg[:], op0=ALU.mult, op1=ALU.mult),
             reads=[('ssG', i), ('xg', i % 3), 'fg'], writes=[('xg', i % 3)])
        S.dma('sp', lambda e: e.dma_start(out=out_d[i * 128:(i + 1) * 128, :], in_=xt_[:]), reads=[('xg', i % 3)])

    def g_pipe(i, nb):
        if i < 16:
            g_a(i, nb)
        if 0 <= i - 1 < 16:
            g_b(i - 1)
        if 0 <= i - 2 < 16:
            g_c(i - 2)
    n = 0
    gq = []
    for tb in range(4):
        for j in range(8):
            fstep(tb, j, n)
            if gq and j % 2 == 1:
                g_pipe(gq.pop(0), ((n + 1) % 2) * 4 + 2)
            n += 1
        gq += list(range(4 * tb, 4 * tb + 4))
    k_ = 0
    for i in gq + [16, 17]:
        g_pipe(i, (k_ % 2) * 4 + 2)
        k_ += 1
    S.flush()

    nc._dbg_out = dbg_out
    nc._arena_peak = AR.peak
    return nc, din, consts


def _run(inputs, dbg=(), stop=None):
    nc, din, consts = build(dbg, stop)
    shared = _layout_shared(inputs)
    shared.update(consts)
    x = np.asarray(inputs["x"], np.float32); ctx = np.asarray(inputs["ctx"], np.float32)
    c = np.asarray(inputs["c"], np.float32); c_ctx = np.asarray(inputs["c_ctx"], np.float32)
    in_maps = []
    for b in range(8):
        m = dict(shared)
        m["x"] = np.ascontiguousarray(x[b]); m["ctx"] = np.ascontiguousarray(ctx[b])
        cc = np.stack([c[b], c_ctx], axis=-1)
        m["cT"] = np.ascontiguousarray(cc.reshape(8, 128, 2).transpose(1, 0, 2))
        in_maps.append(m)
    res = run_bass_kernel_spmd(nc, in_maps, core_ids=list(range(8)))
    return res


def kernel(**inputs):
    res = _run(inputs)
    return np.stack([np.asarray(r["out"], np.float32) for r in res.results], axis=0)
```

```python
import math
import numpy as np
import ml_dtypes
import concourse.bass as bass
import concourse.mybir as mybir
from concourse.bass_utils import run_bass_kernel_spmd
from concourse.alu_op_type import AluOpType as ALU

F32 = mybir.dt.float32
BF16 = mybir.dt.bfloat16
I32 = mybir.dt.int32
AF = mybir.ActivationFunctionType
AX = mybir.AxisListType

ENGS = ['pe', 'act', 'dve', 'pool', 'sp']
EPS = 1e-6
NLAT = 2048
NCTX = 256
NTOK = NLAT + NCTX
NCH = NTOK // 8
D = 1024


class _Op:
    __slots__ = ('eng', 'fn', 'deps', 'pos', 'needed', 'sig', 'is_dma', 'dsem', 'dval', 'waits')


class Sched:
    def __init__(self, nc, n_dma_sems=16):
        self.nc = nc
        self.sem = {e: nc.alloc_semaphore(name=f"sem_{e}") for e in ENGS}
        self.dq = {}
        for q in ('sp', 'pool'):
            self.dq[q] = dict(sems=[nc.alloc_semaphore(name=f"dsem_{q}{i}") for i in range(n_dma_sems)],
                              cnt=[0] * n_dma_sems, last=[None] * n_dma_sems, rr=0)
        self.pending = {e: [] for e in ENGS}
        self.npos = {e: 0 for e in ENGS}
        self.nsig = {e: 0 for e in ENGS}
        self.regions = {}
        self.block_dmas = []

    def _mk(self, eng, fn, is_dma):
        op = _Op()
        op.eng = eng; op.fn = fn; op.deps = []; op.needed = False; op.sig = None
        op.is_dma = is_dma; op.dsem = None; op.dval = None; op.waits = None
        op.pos = self.npos[eng]; self.npos[eng] += 1
        self.pending[eng].append(op)
        return op

    def _add_deps(self, op, reads, writes):
        deps = op.deps
        for k in reads:
            r = self.regions.get(k)
            if r is None:
                r = self.regions[k] = [None, []]
            w = r[0]
            if w is not None and (w.is_dma or op.is_dma or w.eng != op.eng or op.eng != 'pe'):
                deps.append(w)
        for k in writes:
            r = self.regions.get(k)
            if r is None:
                r = self.regions[k] = [None, []]
            w = r[0]
            if w is not None and (w.is_dma or op.is_dma or w.eng != op.eng or op.eng != 'pe'):
                deps.append(w)
            for rd in r[1]:
                if rd is not op and (rd.is_dma or op.is_dma or rd.eng != op.eng or op.eng != 'pe'):
                    deps.append(rd)
        for k in reads:
            self.regions[k][1].append(op)
        for k in writes:
            r = self.regions[k]
            r[0] = op; r[1] = []

    def op(self, eng, fn, reads=(), writes=()):
        psr = [k for k in reads if isinstance(k, tuple) and k[0] == 'ps']
        if psr:
            reads = [k for k in reads if not (isinstance(k, tuple) and k[0] == 'ps')]
            writes = list(writes) + psr
        op = self._mk(eng, fn, False)
        self._add_deps(op, reads, writes)
        return op

    def dma(self, q, fn, reads=(), writes=()):
        op = self._mk(q, fn, True)
        d = self.dq[q]
        i = d['rr']; d['rr'] = (i + 1) % len(d['sems'])
        prev = d['last'][i]
        if prev is not None:
            op.deps.append(prev)
        d['cnt'][i] += 1
        op.dsem = d['sems'][i]; op.dval = 16 * d['cnt'][i]
        d['last'][i] = op
        self._add_deps(op, reads, writes)
        self.block_dmas.append(op)
        return op

    def flush(self):
        nc = self.nc
        tail = {}
        for e in ENGS:
            waited = {}
            for op in self.pending[e]:
                ws = []
                for dp in op.deps:
                    if dp.is_dma:
                        key = ('d', id(dp.dsem))
                        if waited.get(key, 0) >= dp.dval:
                            continue
                        waited[key] = dp.dval
                        ws.append(dp)
                    else:
                        key = ('c', dp.eng)
                        if waited.get(key, -1) >= dp.pos:
                            continue
                        waited[key] = dp.pos
                        dp.needed = True
                        ws.append(dp)
                op.waits = ws
            tail[e] = waited
        for e in ENGS:
            for op in self.pending[e]:
                if (not op.is_dma) and op.needed:
                    self.nsig[e] += 1
                    op.sig = self.nsig[e]
        engobj = {'pe': 'tensor', 'act': 'scalar', 'dve': 'vector', 'pool': 'gpsimd', 'sp': 'sync'}
        sched = self

        def replay(e, eng):
            for op in sched.pending[e]:
                for dp in op.waits:
                    if dp.is_dma:
                        eng.wait_ge(dp.dsem, dp.dval)
                    else:
                        eng.wait_ge(sched.sem[dp.eng], dp.sig)
                ins = op.fn(eng)
                if op.is_dma:
                    ins.then_inc(op.dsem, 16)
                elif op.needed:
                    ins.then_inc(sched.sem[e], 1)
            for op in sched.block_dmas:
                if op.eng == e:
                    key = ('d', id(op.dsem))
                    if tail[e].get(key, 0) < op.dval:
                        tail[e][key] = op.dval
                        eng.wait_ge(op.dsem, op.dval)

        with nc.Block() as block:
            for e in ENGS:
                if not sched.pending[e]:
                    continue
                getattr(block, engobj[e])(lambda eng, e=e: replay(e, eng))
        self.pending = {e: [] for e in ENGS}
        self.regions = {}
        self.block_dmas = []


class Arena:
    def __init__(self, nc, lo, hi):
        self.nc = nc; self.lo = lo; self.hi = hi; self.cur = lo; self.n = 0; self.peak = lo

    def alloc(self, name, shape, dt):
        esz = {F32: 4, BF16: 2, I32: 4}[dt]
        nbytes = esz * int(np.prod(shape[1:]))
        off = (self.cur + 63) // 64 * 64
        assert off + nbytes <= self.hi, f"SBUF arena overflow at {name}: {off + nbytes} > {self.hi}"
        self.cur = off + nbytes
        self.peak = max(self.peak, self.cur)
        self.n += 1
        self.offs = getattr(self, 'offs', {}); self.offs[name] = off
        return self.nc.alloc_sbuf_tensor_at(f"{name}_{self.n}", list(shape), dt, offset=off)

    def mark(self):
        return self.cur

    def release(self, m):
        self.cur = m


def _bf(a):
    return np.ascontiguousarray(a.astype(ml_dtypes.bfloat16))


def _constants():
    c = {}
    c["ident_f"] = np.eye(128, dtype=np.float32)
    c["ident_b"] = _bf(np.eye(128, dtype=np.float32))
    n = NLAT
    rows = n // 64
    row_ids = np.repeat(np.arange(rows, dtype=np.float32), 64)
    col_ids = np.tile(np.arange(64, dtype=np.float32), rows)
    freqs = (np.float32(10000.0) ** (-np.arange(16, dtype=np.float32) / np.float32(16))).astype(np.float32)
    ar = (row_ids[:, None] * freqs).astype(np.float32)
    ac = (col_ids[:, None] * freqs).astype(np.float32)
    cr, sr = np.cos(ar).astype(np.float32), np.sin(ar).astype(np.float32)
    cc, sc = np.cos(ac).astype(np.float32), np.sin(ac).astype(np.float32)
    C64 = np.concatenate([cr, cr, cc, cc], axis=1)
    S64 = np.concatenate([-sr, sr, -sc, sc], axis=1)
    Cf = np.ones((NTOK, 64), np.float32); Sf = np.zeros((NTOK, 64), np.float32)
    Cf[:n] = C64; Sf[:n] = S64
    c["ropeC"] = np.ascontiguousarray(Cf.reshape(18, 128, 64).transpose(1, 0, 2))
    c["ropeS"] = np.ascontiguousarray(Sf.reshape(18, 128, 64).transpose(1, 0, 2))
    sel = np.zeros((128, 8, 8, 128), np.float32)
    for tok in range(128):
        for j in range(8):
            sel[tok, j, tok % 8, 16 * j + tok // 8] = 1.0
    c["sel"] = _bf(sel)
    selT = np.zeros((128, 8, 512), np.float32)
    for cc_ in range(128):
        for s in range(8):
            selT[cc_, s, 8 * (cc_ % 64) + s] = 1.0
    c["selT"] = _bf(selT)
    s_idx = np.arange(128) // 16
    c["maskF"] = (s_idx[None, :] >= s_idx[:, None]).astype(np.float32)
    c["maskB"] = (s_idx[None, :] <= s_idx[:, None]).astype(np.float32)
    col = np.arange(NCH)
    posF = np.where(col < 256, col + 32, col - 256).astype(np.float32)
    posB = (NCH - 1 - col).astype(np.float32)
    c["pos"] = np.ascontiguousarray(np.broadcast_to(np.stack([posF, posB])[None], (128, 2, NCH))).astype(np.float32)
    return c


def _layout_shared(inp):
    f = lambda a: np.ascontiguousarray(np.asarray(a, dtype=np.float32))
    s = {}
    s["w_ada"] = f(inp["w_ada"][0])
    s["b_adaT"] = f(inp["b_ada"][0].reshape(24, 128).T)
    s["b_gate_bc"] = f(np.broadcast_to(inp["b_ada"][0][2048:3072][None], (128, 1024)))
    s["b_ada2"] = f(np.broadcast_to(inp["b_ada"][0][None], (2, 3072)))
    s["norm_gT"] = f(inp["norm_g"][0].reshape(8, 128).T)
    s["w_in"] = f(inp["w_in"][0])
    s["qg_bc"] = f(np.broadcast_to(inp["q_norm_g"][0][None], (128, 64)))
    s["kg_bc"] = f(np.broadcast_to(inp["k_norm_g"][0][None], (128, 64)))

    def gq(a):
        a = np.asarray(a, np.float32)
        rest = a.shape[3:]
        a = a.reshape((2, 16, 2, 64) + rest)
        a = np.moveaxis(a, (2, 3), (0, 1))
        return f(a.reshape((128, 32) + rest))
    s["lamre"] = gq(inp["s5_lam_re"][0])
    s["lamim"] = gq(inp["s5_lam_im"][0])
    s["logdt"] = gq(np.broadcast_to(np.asarray(inp["s5_log_dt"][0])[:, :, None], (2, 32, 64)))
    s["bre"] = gq(inp["s5_b_re"][0])
    s["bim"] = gq(inp["s5_b_im"][0])
    s["cre"] = gq(np.transpose(np.asarray(inp["s5_c_re"][0]), (0, 1, 3, 2)))
    s["cim"] = gq(np.transpose(np.asarray(inp["s5_c_im"][0]), (0, 1, 3, 2)))
    d = np.asarray(inp["s5_d"][0], np.float32).reshape(32, 16)
    s["dcol"] = f(np.tile(d.T, (8, 1)))
    s["w_glu"] = f(inp["w_glu"][0])
    s["b_gluT"] = f(inp["b_glu"][0].reshape(4, 128).T)
    s["w_bra"] = f(inp["w_branch_attn"][0])
    s["w_brb"] = f(inp["w_branch_s5"][0])
    s["w_out"] = f(inp["w_out"][0])
    s["fg_bc"] = f(np.broadcast_to(np.asarray(inp["final_norm_g"])[None], (128, 1024)))
    return s


_SPECS = None


def build(dbg=(), stop=None):
    nc = bass.Bass('TRN2', target_bir_lowering=False)
    S = Sched(nc)
    consts = _constants()
    din = {}

    def dram_in(name, shape, dt=F32):
        din[name] = nc.dram_tensor(name, list(shape), dt, kind="ExternalInput").ap()
        return din[name]

    x_d = dram_in("x", [NLAT, D]); ctx_d = dram_in("ctx", [NCTX, D]); cT_d = dram_in("cT", [128, 8, 2])
    shared_shapes = dict(w_ada=[1024, 3072], b_adaT=[128, 24], b_gate_bc=[128, 1024], b_ada2=[2, 3072], norm_gT=[128, 8],
                         w_in=[1024, 4352], qg_bc=[128, 64], kg_bc=[128, 64], lamre=[128, 32], lamim=[128, 32],
                         logdt=[128, 32], bre=[128, 32, 16], bim=[128, 32, 16], cre=[128, 32, 16], cim=[128, 32, 16],
                         dcol=[128, 32], w_glu=[512, 512], b_gluT=[128, 4], w_bra=[512, 1024], w_brb=[512, 1024],
                         w_out=[1024, 1024], fg_bc=[128, 1024])
    for k, shp in shared_shapes.items():
        dram_in(k, shp)
    for k, v in consts.items():
        dram_in(k, v.shape, BF16 if v.dtype == ml_dtypes.bfloat16 else F32)
    out_d = nc.dram_tensor("out", [NLAT, D], F32, kind="ExternalOutput").ap()
    dbg_out = {}

    def dump(name, tens_ap, shape, dt=F32, reads=()):
        if name not in dbg:
            return
        o = nc.dram_tensor("dbg_" + name, list(shape), dt, kind="ExternalOutput").ap()
        dbg_out[name] = o
        S.dma('sp', lambda e: e.dma_start(out=o, in_=tens_ap), reads=list(reads))

    PSP = [nc.alloc_psum_tensor(f"psp{i}", [128, 1024], F32) for i in range(4)]
    PS = [PSP[i // 2][:, (i % 2) * 512:(i % 2 + 1) * 512] for i in range(8)]
    AR = Arena(nc, 16512, 229344)
    HI_WTM = 229344 - 20480
    HI_WGA = HI_WTM - 8192
    wtm = nc.alloc_sbuf_tensor_at("wtm_hi", [128, 8, 1280], BF16, offset=HI_WTM)
    wga = nc.alloc_sbuf_tensor_at("wga_hi", [128, 8, 512], BF16, offset=HI_WGA)
    AR.hi = HI_WGA

    ident_f = AR.alloc("ident_f", [128, 128], F32)
    ident_b = AR.alloc("ident_b", [128, 128], BF16)
    mod = AR.alloc("mod", [128, 24, 2], F32)
    modA = AR.alloc("modA", [128, 2, 8], F32)
    gate_bc = AR.alloc("gate_bc", [128, 1024], F32)
    xnT = AR.alloc("xnT", [128, 8, NTOK], BF16)
    S.dma('sp', lambda e: e.dma_start(out=ident_f[:], in_=din["ident_f"]), writes=['ident_f'])
    S.dma('sp', lambda e: e.dma_start(out=ident_b[:], in_=din["ident_b"]), writes=['ident_b'])
    m_persist = AR.mark()

    scT = AR.alloc("scT", [128, 8, 2], F32)
    screp = AR.alloc("screp", [128, 8, 128], F32)
    b_adaT = AR.alloc("b_adaT", [128, 24], F32)
    norm_gT = AR.alloc("norm_gT", [128, 8], F32)
    bgate = AR.alloc("bgate", [128, 1024], F32)
    wa = [AR.alloc(f"wa{i}", [128, 3072], F32) for i in range(2)]
    S.dma('sp', lambda e: e.dma_start(out=scT[:], in_=din["cT"]), writes=['scT'])
    S.dma('sp', lambda e: e.dma_start(out=b_adaT[:], in_=din["b_adaT"]), writes=['b_adaT'])
    S.dma('sp', lambda e: e.dma_start(out=norm_gT[:], in_=din["norm_gT"]), writes=['norm_gT'])
    S.dma('sp', lambda e: e.dma_start(out=bgate[:], in_=din["b_gate_bc"]), writes=['bgate'])
    bada2 = AR.alloc("bada2", [2, 3072], F32)
    modrow = AR.alloc("modrow", [2, 3072], F32)
    ones1 = AR.alloc("ones1", [1, 128], F32)
    S.dma('sp', lambda e: e.dma_start(out=bada2[:], in_=din["b_ada2"]), writes=['bada2'])
    S.op('pool', lambda e: e.memset(ones1[:], 1.0), writes=['ones1'])
    S.op('act', lambda e: e.activation(out=scT[:], in_=scT[:], func=AF.Silu), reads=['scT'], writes=['scT'])
    S.op('dve', lambda e: e.tensor_copy(out=screp[:], in_=scT[:, :, 0:1].to_broadcast([128, 8, 128])),
         reads=['scT'], writes=['screp'])
    xt = [AR.alloc(f"xt{i}", [128, 1024], F32) for i in range(18)]
    junk = AR.alloc("junkB", [128, 1024], F32)
    ssB = AR.alloc("ssB", [128, 18], F32)
    rsB = AR.alloc("rsB", [128, 18], F32)

    def b_stage1(i):
        t = xt[i]
        src = x_d[i * 128:(i + 1) * 128, :] if i < 16 else ctx_d[(i - 16) * 128:(i - 15) * 128, :]
        S.dma('pool', lambda e: e.dma_start(out=t[:], in_=src), writes=[('xt', i)])
        S.op('act', lambda e: e.activation(out=junk[:], in_=t[:], func=AF.Square, accum_out=ssB[:, i:i + 1]),
             reads=[('xt', i)], writes=['junkB', ('ssB', i)])
        S.op('dve', lambda e: e.tensor_scalar(out=rsB[:, i:i + 1], in0=ssB[:, i:i + 1], scalar1=1.0 / D, scalar2=EPS,
                                              op0=ALU.mult, op1=ALU.add), reads=[('ssB', i)], writes=[('rsB', i)])

    def b_stage2(i):
        t = xt[i]
        S.op('act', lambda e: e.activation(out=rsB[:, i:i + 1], in_=rsB[:, i:i + 1], func=AF.Sqrt),
             reads=[('rsB', i)], writes=[('rsB', i)])
        S.op('dve', lambda e: e.reciprocal(out=rsB[:, i:i + 1], in_=rsB[:, i:i + 1]),
             reads=[('rsB', i)], writes=[('rsB', i)])
        S.op('act', lambda e: e.activation(out=t[:], in_=t[:], func=AF.Copy, scale=rsB[:, i:i + 1]),
             reads=[('xt', i), ('rsB', i)], writes=[('xt', i)])
    b_order = []
    for i in range(18):
        b_order.append((1, i))
        if i >= 1:
            b_order.append((2, i - 1))
    b_order.append((2, 17))
    b_pos = [0]

    def b_emit(n):
        for _ in range(n):
            if b_pos[0] < len(b_order):
                st, i = b_order[b_pos[0]]; b_pos[0] += 1
                (b_stage1 if st == 1 else b_stage2)(i)
    for kt in range(8):
        w = wa[kt % 2]
        b_emit(5)
        S.dma('sp', lambda e, w=w, kt=kt: e.dma_start(out=w[:], in_=din["w_ada"][kt * 128:(kt + 1) * 128, :]),
              writes=[('wa', kt % 2)])
        for c6 in range(6):
            S.op('pe', lambda e, w=w, c6=c6, kt=kt: e.matmul(
                PS[c6][0:2, :], lhsT=scT[:, kt, :], rhs=w[:, c6 * 512:(c6 + 1) * 512], start=(kt == 0), stop=(kt == 7)),
                reads=[('wa', kt % 2), 'scT'], writes=[('ps', c6)])
    for c6 in range(6):
        S.op('dve', lambda e, c6=c6: e.tensor_tensor(out=modrow[:, c6 * 512:(c6 + 1) * 512], in0=PS[c6][0:2, :],
                                                     in1=bada2[:, c6 * 512:(c6 + 1) * 512], op=ALU.add),
             reads=[('ps', c6), 'bada2'], writes=[('modrow', c6)])
    for j in range(24):
        S.op('pe', lambda e, j=j: e.transpose(out=PS[6][:, 2 * j:2 * j + 2], in_=modrow[0:2, j * 128:(j + 1) * 128],
                                              identity=ident_f[0:2, 0:2]),
             reads=[('modrow', j // 4), 'ident_f'], writes=[('ps', 6)])
    S.op('dve', lambda e: e.tensor_copy(out=mod[:].rearrange("p j n -> p (j n)"), in_=PS[6][:, 0:48]),
         reads=[('ps', 6)], writes=['mod'])
    for c2 in range(2):
        S.op('pe', lambda e, c2=c2: e.matmul(PS[7][:, :], lhsT=ones1[0:1, :], rhs=modrow[0:1, 2048 + c2 * 512:2048 + (c2 + 1) * 512],
                                             start=True, stop=True),
             reads=[('modrow', 4 + c2), 'ones1'], writes=[('ps', 7)])
        S.op('dve', lambda e, c2=c2: e.tensor_copy(out=gate_bc[:, c2 * 512:(c2 + 1) * 512], in_=PS[7][:, :]),
             reads=[('ps', 7)], writes=[('gate_bc', c2)])
    for n in range(2):
        S.op('dve', lambda e, n=n: e.scalar_tensor_tensor(out=modA[:, n, :], in0=mod[:, 8:16, n], scalar=1.0,
                                                          in1=norm_gT[:], op0=ALU.add, op1=ALU.mult),
             reads=['mod', 'norm_gT'], writes=['modA'])
    b_emit(100)
    for kt in range(8):
        S.dma('pool', lambda e, kt=kt: e.dma_start(out=wtm[:, kt, 0:768], in_=din["w_in"][kt * 128:(kt + 1) * 128, 0:768]),
              writes=[('wtm', kt)])
        S.dma('pool', lambda e, kt=kt: e.dma_start(out=wtm[:, kt, 768:1280], in_=din["w_in"][kt * 128:(kt + 1) * 128, 1280:1792]),
              writes=[('wtm', kt)])
    dump("mod", mod[:], [128, 24, 2], reads=['mod'])
    dump("gate_bc", gate_bc[:], [128, 1024], reads=[('gate_bc', 0), ('gate_bc', 1)])

    for i in range(18):
        t = xt[i]
        n = 0 if i < 16 else 1
        for half in range(2):
            bk = (i % 2) * 2 + half + 4
            pb = PS[bk]
            for k4 in range(4):
                kt = half * 4 + k4
                S.op('pe', lambda e, t=t, pb=pb, k4=k4, kt=kt: e.transpose(
                    out=pb[:, k4 * 128:(k4 + 1) * 128], in_=t[:, kt * 128:(kt + 1) * 128], identity=ident_f[:]),
                    reads=[('xt', i), 'ident_f'], writes=[('ps', bk)])
            for k4 in range(4):
                kt = half * 4 + k4
                S.op('dve', lambda e, pb=pb, k4=k4, kt=kt, i=i, n=n: e.tensor_scalar(
                    out=xnT[:, kt, i * 128:(i + 1) * 128], in0=pb[:, k4 * 128:(k4 + 1) * 128],
                    scalar1=modA[:, n, kt:kt + 1], scalar2=mod[:, kt, n:n + 1], op0=ALU.mult, op1=ALU.add),
                    reads=[('ps', bk), 'modA', 'mod'], writes=[('xnT', kt, i)])
    dump("xnT", xnT[:], [128, 8, NTOK], BF16, reads=[('xnT', kt, i) for kt in range(8) for i in range(18)])
    S.flush()
    AR.release(m_persist)
    if stop == 'B':
        return nc, din, consts

    ygT = AR.alloc("ygT", [128, 4, NLAT], BF16)
    utm = AR.alloc("utm", [128, 18, 512], BF16)
    m_mid2 = AR.mark()
    qT = AR.alloc("qT", [128, 4, NLAT], BF16)
    kTz = AR.alloc("kTz", [128, 2, 2, NTOK], BF16)
    Vaug = AR.alloc("Vaug", [128, 18, 2, 65], BF16)
    m_mid = AR.mark()
    ropeC = AR.alloc("ropeC", [128, 18, 64], F32)
    ropeS = AR.alloc("ropeS", [128, 18, 64], F32)
    qg = AR.alloc("qg", [128, 64], F32)
    kg = AR.alloc("kg", [128, 64], F32)
    GCq = AR.alloc("GCq", [128, 18, 64], F32); GSq = AR.alloc("GSq", [128, 18, 64], F32)
    GCk = AR.alloc("GCk", [128, 18, 64], F32); GSk = AR.alloc("GSk", [128, 18, 64], F32)
    sqs_ = [AR.alloc(f"sqs{i}", [128, 640], F32) for i in range(2)]
    t1_ = [AR.alloc(f"t1{i}", [128, 640], F32) for i in range(2)]
    t2_ = [AR.alloc(f"t2{i}", [128, 640], F32) for i in range(2)]
    ssq = AR.alloc("ssq", [128, 18, 10], F32)
    qtok = [AR.alloc(f"qtok{i}", [128, 512], BF16) for i in range(2)]
    kpad = [AR.alloc(f"kpad{i}", [128, 2, 2, 2, 64], BF16) for i in range(2)]
    for kp_ in kpad:
        S.op('pool', lambda e, kp_=kp_: e.memset(kp_[:], 0.0), writes=[('kpad_init', id(kp_))])
    for kt in range(8):
        S.dma('pool', lambda e, kt=kt: e.dma_start(out=wga[:, kt, :], in_=din["w_in"][kt * 128:(kt + 1) * 128, 768:1280]),
              writes=[('wga', kt)])
    S.dma('sp', lambda e: e.dma_start(out=ropeC[:], in_=din["ropeC"]), writes=['ropeC'])
    S.dma('sp', lambda e: e.dma_start(out=ropeS[:], in_=din["ropeS"]), writes=['ropeS'])
    S.dma('sp', lambda e: e.dma_start(out=qg[:], in_=din["qg_bc"]), writes=['qg'])
    S.dma('sp', lambda e: e.dma_start(out=kg[:], in_=din["kg_bc"]), writes=['kg'])
    S.op('pool', lambda e: e.memset(Vaug[:], 1.0), writes=['Vaug_init'])
    for (g_, GC_, GS_, nm) in ((qg, GCq, GSq, 'q'), (kg, GCk, GSk, 'k')):
        S.op('pool', lambda e, g_=g_, GC_=GC_: e.tensor_tensor(out=GC_[:], in0=ropeC[:], in1=g_[:].unsqueeze(1).to_broadcast([128, 18, 64]),
                                                              op=ALU.mult), reads=['ropeC', nm + 'g'], writes=['GC' + nm])
        gv = g_[:].rearrange("p (r j f) -> p r j f", r=2, j=2)
        Sv = ropeS[:].rearrange("p i (r j f) -> p i r j f", r=2, j=2)
        Gv = GS_[:].rearrange("p i (r j f) -> p i r j f", r=2, j=2)
        for j in range(2):
            S.op('pool', lambda e, j=j, gv=gv, Sv=Sv, Gv=Gv: e.tensor_tensor(
                out=Gv[:, :, :, j, :], in0=Sv[:, :, :, j, :],
                in1=gv[:, :, 1 - j, :].unsqueeze(1).to_broadcast([128, 18, 2, 16]), op=ALU.mult),
                reads=['ropeS', nm + 'g'], writes=[('GS' + nm, j)])

    import os as _os
    _cs = _os.environ.get("CSTOP", "")
    if _cs == "setup":
        dump("GSq", GSq[:], [128, 18, 64], reads=[('GSq', 0), ('GSq', 1)])
        S.flush()
        return nc, din, consts

    def head_norm_rope(i, pv, bk, nh, GC_, GS_, nm, out3, so):
        W = nh * 64
        sl = slice(so // 64, so // 64 + nh)
        par = i % 2
        sqs = sqs_[par]; t1 = t1_[par]; t2 = t2_[par]
        nm_ = nm
        nm = (nm_, par)
        S.op('act', lambda e: e.activation(out=sqs[:, so:so + W], in_=pv, func=AF.Square), reads=[('ps', bk)], writes=[('sqs', nm)])
        yield
        p3 = pv.rearrange("p (h d) -> p h d", d=64)
        S.op('dve', lambda e: e.tensor_tensor(out=t1[:, so:so + W].rearrange("p (h d) -> p h d", d=64), in0=p3,
                                              in1=GC_[:, i, :].unsqueeze(1).to_broadcast([128, nh, 64]), op=ALU.mult),
             reads=[('ps', bk), 'GC' + nm_], writes=[('t1', nm)])
        yield
        S.op('dve', lambda e: e.tensor_reduce(out=ssq[:, i, sl], in_=sqs[:, so:so + W].rearrange("p (h d) -> p h d", d=64),
                                              axis=AX.X, op=ALU.add), reads=[('sqs', nm)], writes=[('ssq', i, nm)])
        yield
        S.op('dve', lambda e: e.tensor_scalar(out=ssq[:, i, sl], in0=ssq[:, i, sl], scalar1=1.0 / 64, scalar2=EPS,
                                              op0=ALU.mult, op1=ALU.add), reads=[('ssq', i, nm)], writes=[('ssq', i, nm)])
        yield
        S.op('act', lambda e: e.activation(out=ssq[:, i, sl], in_=ssq[:, i, sl], func=AF.Sqrt),
             reads=[('ssq', i, nm)], writes=[('ssq', i, nm)])
        yield
        p5 = pv.rearrange("p (h r j f) -> p h r j f", r=2, j=2, f=16)
        t5 = t2[:, so:so + W].rearrange("p (h r j f) -> p h r j f", r=2, j=2, f=16)
        G5 = GS_[:, i, :].rearrange("p (r j f) -> p r j f", r=2, j=2)
        for j in range(2):
            S.op('dve', lambda e, j=j: e.tensor_tensor(
                out=t5[:, :, :, j, :], in0=p5[:, :, :, 1 - j, :],
                in1=G5[:, :, j, :].unsqueeze(1).to_broadcast([128, nh, 2, 16]), op=ALU.mult),
                reads=[('ps', bk), ('GS' + nm_, 0), ('GS' + nm_, 1)], writes=[('t2', nm, j)])
            yield
        S.op('pool', lambda e: e.tensor_tensor(out=t1[:, so:so + W], in0=t1[:, so:so + W], in1=t2[:, so:so + W], op=ALU.add),
             reads=[('t1', nm), ('t2', nm, 0), ('t2', nm, 1)], writes=[('t1', nm)])
        yield
        S.op('dve', lambda e: e.reciprocal(out=ssq[:, i, sl], in_=ssq[:, i, sl]), reads=[('ssq', i, nm)], writes=[('ssq', i, nm)])
        yield
        S.op('dve', lambda e: e.tensor_tensor(out=out3, in0=t1[:, so:so + W].rearrange("p (h d) -> p h d", d=64),
                                              in1=ssq[:, i, sl].unsqueeze(2).to_broadcast([128, nh, 64]), op=ALU.mult),
             reads=[('t1', nm), ('ssq', i, nm)], writes=[(nm_ + 'tok', i % 2)])
        yield

    def tileC_mm(i):
        lat = i < 16
        s4 = (i % 2) * 4
        bq, bkv, bu, bt = s4, s4 + 1, s4 + 2, s4 + 3
        for kt in range(8):
            lhs = xnT[:, kt, i * 128:(i + 1) * 128]
            if lat:
                S.op('pe', lambda e, kt=kt, lhs=lhs: e.matmul(PS[bq][:, :], lhsT=lhs, rhs=wtm[:, kt, 0:512], start=(kt == 0), stop=(kt == 7)),
                     reads=[('xnT', kt, i), ('wtm', kt)], writes=[('ps', bq)])
            S.op('pe', lambda e, kt=kt, lhs=lhs: e.matmul(PS[bkv][:, 0:256], lhsT=lhs, rhs=wtm[:, kt, 512:768], start=(kt == 0), stop=(kt == 7)),
                 reads=[('xnT', kt, i), ('wtm', kt)], writes=[('ps', bkv)])
            S.op('pe', lambda e, kt=kt, lhs=lhs: e.matmul(PS[bu][:, :], lhsT=lhs, rhs=wtm[:, kt, 768:1280], start=(kt == 0), stop=(kt == 7)),
                 reads=[('xnT', kt, i), ('wtm', kt)], writes=[('ps', bu)])

    def tileC(i):
        lat = i < 16
        s4 = (i % 2) * 4
        bq, bkv, bu, bt = s4, s4 + 1, s4 + 2, s4 + 3
        kd = kpad[i % 2]
        gens = []
        if lat:
            gens.append(head_norm_rope(i, PS[bq][:, :], bq, 8, GCq, GSq, 'q', qtok[i % 2][:].rearrange("p (h d) -> p h d", d=64), 0))
        gens.append(head_norm_rope(i, PS[bkv][:, 0:128], bkv, 2, GCk, GSk, 'k', kd[:, :, 0, 0, :], 512))
        while gens:
            for g_ in list(gens):
                try:
                    next(g_)
                except StopIteration:
                    gens.remove(g_)
        S.op('pool', lambda e, kd=kd: e.tensor_copy(out=kd[:, :, 1, 1, :], in_=kd[:, :, 0, 0, :]),
             reads=[('ktok', i % 2), ('kpad_init', id(kd))], writes=[('kdup', i % 2)])
        S.op('act', lambda e, i=i: e.activation(out=Vaug[:, i, :, 0:64], in_=PS[bkv][:, 128:256].rearrange("p (h d) -> p h d", d=64),
                                                func=AF.Copy), reads=[('ps', bkv), 'Vaug_init'], writes=[('Vaug', i)])
        S.op('act', lambda e, i=i: e.activation(out=utm[:, i, :], in_=PS[bu][:, :], func=AF.Copy), reads=[('ps', bu)], writes=[('utm', i)])
        ptb = PS[bt][:].bitcast(BF16)
        if lat:
            for hp in range(4):
                S.op('pe', lambda e, hp=hp, i=i, ptb=ptb: e.transpose(out=ptb[:, hp * 128:(hp + 1) * 128],
                                                                      in_=qtok[i % 2][:, hp * 128:(hp + 1) * 128], identity=ident_b[:]),
                     reads=[('qtok', i % 2), 'ident_b'], writes=[('ps', bt)])
        for kv in range(2):
            for h2 in range(2):
                S.op('pe', lambda e, kv=kv, h2=h2, kd=kd, ptb=ptb: e.transpose(
                    out=ptb[:, 512 + (kv * 2 + h2) * 128:512 + (kv * 2 + h2 + 1) * 128],
                    in_=kd[:, kv, h2, :, :].rearrange("p a d -> p (a d)"), identity=ident_b[:]),
                    reads=[('ktok', i % 2), ('kdup', i % 2), 'ident_b', ('kpad_init', id(kd))], writes=[('ps', bt)])
        if lat:
            S.op('act', lambda e, i=i, ptb=ptb: e.activation(out=qT[:, :, i * 128:(i + 1) * 128],
                                                             in_=ptb[:, 0:512].rearrange("p (a t) -> p a t", t=128), func=AF.Copy),
                 reads=[('ps', bt)], writes=[('qT', i)])
        S.op('dve', lambda e, i=i, ptb=ptb: e.tensor_copy(out=kTz[:, :, :, i * 128:(i + 1) * 128],
                                                         in_=ptb[:, 512:1024].rearrange("p (a b t) -> p a b t", a=2, t=128)),
             reads=[('ps', bt)], writes=[('kTz', i)])
    tileC_mm(0)
    for i in range(18):
        if i + 1 < 18:
            tileC_mm(i + 1)
        tileC(i)
    dump("qT", qT[:], [128, 4, NLAT], BF16, reads=[('qT', i) for i in range(16)])
    dump("Vaug", Vaug[:], [128, 18, 2, 65], BF16, reads=[('Vaug', i) for i in range(18)])
    dump("utm", utm[:], [128, 18, 512], BF16, reads=[('utm', i) for i in range(18)])
    S.flush()
    AR.release(m_mid)
    if stop == 'C':
        return nc, din, consts

    sgT = AR.alloc("sgT", [128, 8, NLAT], BF16)
    pT2 = [AR.alloc(f"pT{i}", [128, 1024], BF16) for i in range(3)]
    osb = [AR.alloc(f"osb{i}", [128, 512], F32) for i in range(2)]
    rsum = AR.alloc("rsum", [128, 512], F32)
    tmpD = AR.alloc("tmpD", [128, 512], F32)
    ones_f = AR.alloc("ones_f", [128, 64], F32)
    S.op('pool', lambda e: e.memset(ones_f[:], 1.0), writes=['ones_f'])
    for hp in range(4):
        for tb in range(4):
            bk = (hp * 4 + tb) % 2
            for kt in range(8):
                S.op('pe', lambda e, kt=kt, hp=hp, tb=tb, bk=bk: e.matmul(
                    PS[bk][:, :], lhsT=wga[:, kt, hp * 128:(hp + 1) * 128], rhs=xnT[:, kt, tb * 512:(tb + 1) * 512],
                    start=(kt == 0), stop=(kt == 7)), reads=[('wga', kt)], writes=[('ps', bk)])
            for h2 in range(2):
                S.op('act', lambda e, hp=hp, tb=tb, bk=bk, h2=h2: e.activation(
                    out=sgT[0:64, 2 * hp + h2, tb * 512:(tb + 1) * 512], in_=PS[bk][h2 * 64:(h2 + 1) * 64, :], func=AF.Silu),
                    reads=[('ps', bk)], writes=[('sgT', 2 * hp + h2, tb)])
    PI = math.pi
    hp_cur = [HI_WTM]

    def HP(name, shape):
        nbytes = 4 * int(np.prod(shape[1:]))
        off = (hp_cur[0] + 63) // 64 * 64
        hp_cur[0] = off + nbytes
        assert hp_cur[0] <= 229344
        return nc.alloc_sbuf_tensor_at(name + "_hp", list(shape), F32, offset=off)
    rho = HP("rho", [128, 32]); phi = HP("phi", [128, 32])
    chain_ops = []
    S.op = lambda eng, fn, reads=(), writes=(): chain_ops.append((eng, fn, list(reads), list(writes)))
    def small(nm):
        return AR.alloc(nm, [128, 32], F32)
    lamre = small("lamre"); lamim = small("lamim"); logdt = small("logdt")
    breT = AR.alloc("breT", [128, 32, 16], F32); bimT = AR.alloc("bimT", [128, 32, 16], F32)
    creT = HP("creT", [128, 32, 16]); cimT = HP("cimT", [128, 32, 16])
    dcol = HP("dcol", [128, 32])
    dtt = small("dtt"); lre = small("lre"); aa = small("aa"); th = small("th"); mag = small("mag")
    cc = small("cc"); ss = small("ss"); u1 = small("u1"); u2 = small("u2"); Lre = small("Lre"); Lim = small("Lim")
    nre = small("nre"); den = small("den"); fre = small("fre"); fim = small("fim")
    iLre = small("iLre"); iLim = small("iLim"); halfpi = AR.alloc("halfpi", [128, 1], F32)
    PWr = AR.alloc("PWr", [128, 32, 8], F32); PWi = AR.alloc("PWi", [128, 32, 8], F32)
    NPr = AR.alloc("NPr", [128, 32, 8], F32); NPi = AR.alloc("NPi", [128, 32, 8], F32)
    PWsr = HP("PWsr", [128, 32, 8]); PWsi = HP("PWsi", [128, 32, 8])
    NPsr = HP("NPsr", [128, 32, 8]); NPsi = HP("NPsi", [128, 32, 8])
    bbr = HP("bbr", [128, 32, 16]); bbi = HP("bbi", [128, 32, 16])
    bsr = HP("bsr", [128, 32, 16]); bsi = HP("bsi", [128, 32, 16])

    def D1(fn):
        S.op('dve', fn, reads=['E0'], writes=['E0'])

    def A1(fn):
        S.op('act', fn, reads=['E0'], writes=['E0'])

    def TT(o, a, b, op):
        D1(lambda e: e.tensor_tensor(out=o, in0=a, in1=b, op=op))

    ecst = small("ecst"); xs_ = small("xs_"); x2_ = small("x2_")

    def Ppow(o, base, ex):
        S.op('pool', lambda e: e.tensor_tensor(out=o, in0=base, in1=ex, op=ALU.pow), reads=['E0'], writes=['E0'])

    def horner(o, coefs):
        D1(lambda e: e.tensor_scalar(out=o, in0=x2_[:], scalar1=coefs[0], scalar2=coefs[1], op0=ALU.mult, op1=ALU.add))
        for c_ in coefs[2:]:
            TT(o, o, x2_[:], ALU.mult)
            D1(lambda e, c_=c_: e.tensor_scalar(out=o, in0=o, scalar1=c_, scalar2=None, op0=ALU.add))
    D1(lambda e: e.memset(ecst[:], math.e))
    Ppow(dtt[:], ecst[:], logdt[:])
    D1(lambda e: e.tensor_scalar(out=lre[:], in0=lamre[:], scalar1=-1e-4, scalar2=None, op0=ALU.min))
    TT(aa[:], lre[:], dtt[:], ALU.mult)
    TT(th[:], lamim[:], dtt[:], ALU.mult)
    Ppow(mag[:], ecst[:], aa[:])
    D1(lambda e: e.tensor_scalar(out=u1[:], in0=aa[:], scalar1=8.0, scalar2=None, op0=ALU.mult))
    Ppow(rho[:], ecst[:], u1[:])
    D1(lambda e: e.tensor_scalar(out=xs_[:], in0=th[:], scalar1=1.0 / 16, scalar2=None, op0=ALU.mult))
    TT(x2_[:], xs_[:], xs_[:], ALU.mult)
    horner(ss[:], [1.0 / 362880, -1.0 / 5040, 1.0 / 120, -1.0 / 6, 1.0])
    TT(ss[:], ss[:], xs_[:], ALU.mult)
    horner(cc[:], [-1.0 / 3628800, 1.0 / 40320, -1.0 / 720, 1.0 / 24, -0.5, 1.0])
    for _ in range(4):
        TT(u1[:], cc[:], cc[:], ALU.mult)
        TT(u2[:], ss[:], ss[:], ALU.mult)
        D1(lambda e: e.scalar_tensor_tensor(out=ss[:], in0=cc[:], scalar=2.0, in1=ss[:], op0=ALU.mult, op1=ALU.mult))
        TT(cc[:], u1[:], u2[:], ALU.subtract)
    TT(Lre[:], mag[:], cc[:], ALU.mult)
    TT(Lim[:], mag[:], ss[:], ALU.mult)
    D1(lambda e: e.tensor_scalar(out=phi[:], in0=th[:], scalar1=8.0, scalar2=None, op0=ALU.mult))
    D1(lambda e: e.tensor_scalar(out=nre[:], in0=Lre[:], scalar1=-1.0, scalar2=None, op0=ALU.add))
    TT(den[:], lre[:], lre[:], ALU.mult)
    TT(u1[:], lamim[:], lamim[:], ALU.mult)
    TT(den[:], den[:], u1[:], ALU.add)
    D1(lambda e: e.reciprocal(out=den[:], in_=den[:]))
    TT(fre[:], nre[:], lre[:], ALU.mult)
    TT(u1[:], Lim[:], lamim[:], ALU.mult)
    TT(fre[:], fre[:], u1[:], ALU.add)
    TT(fre[:], fre[:], den[:], ALU.mult)
    TT(fim[:], Lim[:], lre[:], ALU.mult)
    TT(u1[:], nre[:], lamim[:], ALU.mult)
    TT(fim[:], fim[:], u1[:], ALU.subtract)
    TT(fim[:], fim[:], den[:], ALU.mult)
    TT(u1[:], mag[:], mag[:], ALU.mult)
    D1(lambda e: e.reciprocal(out=u1[:], in_=u1[:]))
    TT(iLre[:], Lre[:], u1[:], ALU.mult)
    D1(lambda e: e.scalar_tensor_tensor(out=iLim[:], in0=Lim[:], scalar=-1.0, in1=u1[:], op0=ALU.mult, op1=ALU.mult))

    def cmul(o_r, o_i, a_r, a_i, b_r, b_i, t1, t2):
        TT(t1, a_r, b_r, ALU.mult); TT(t2, a_i, b_i, ALU.mult); TT(o_r, t1, t2, ALU.subtract)
        TT(t1, a_r, b_i, ALU.mult); TT(t2, a_i, b_r, ALU.mult); TT(o_i, t1, t2, ALU.add)

    pwt = {tag: [small(f"pwt_{tag}{k}") for k in range(4)] for tag in ('P', 'N')}

    def cm_mults(tag, a_r, a_i, b_r, b_i):
        t = pwt[tag]
        rd = ['E0', ('pwr', tag), ('pwi', tag)]
        for k, (x_, y_) in enumerate(((a_r, b_r), (a_i, b_i), (a_r, b_i), (a_i, b_r))):
            S.op('dve', lambda e, k=k, x_=x_, y_=y_: e.tensor_tensor(out=t[k][:], in0=x_, in1=y_, op=ALU.mult),
                 reads=rd, writes=[('pwt', tag, k)])

    def cm_comb(tag, o_r, o_i):
        t = pwt[tag]
        S.op('dve', lambda e: e.tensor_tensor(out=o_r, in0=t[0][:], in1=t[1][:], op=ALU.subtract),
             reads=[('pwt', tag, 0), ('pwt', tag, 1)], writes=[('pwr', tag)])
        S.op('dve', lambda e: e.tensor_tensor(out=o_i, in0=t[2][:], in1=t[3][:], op=ALU.add),
             reads=[('pwt', tag, 2), ('pwt', tag, 3)], writes=[('pwi', tag)])
    chains = (('P', PWr, PWi, Lre, Lim), ('N', NPr, NPi, iLre, iLim))
    for (tag, Pr, Pi, Br, Bi) in chains:
        S.op('dve', lambda e, Pr=Pr: e.memset(Pr[:, :, 0:1], 1.0), reads=['E0'], writes=[('pwr', tag)])
        S.op('dve', lambda e, Pi=Pi: e.memset(Pi[:, :, 0:1], 0.0), reads=['E0'], writes=[('pwi', tag)])
    for j in range(1, 8):
        for (tag, Pr, Pi, Br, Bi) in chains:
            cm_mults(tag, Pr[:, :, j - 1], Pi[:, :, j - 1], Br[:], Bi[:])
        for (tag, Pr, Pi, Br, Bi) in chains:
            cm_comb(tag, Pr[:, :, j], Pi[:, :, j])
    PK = [('pwr', 'P'), ('pwi', 'P'), ('pwr', 'N'), ('pwi', 'N')]
    for (src, dst) in ((PWr, PWsr), (PWi, PWsi), (NPr, NPsr), (NPi, NPsi)):
        S.op('dve', lambda e, src=src, dst=dst: e.tensor_copy(out=dst[:, 0:16, :], in_=src[:, 0:16, :]), reads=['E0'] + PK, writes=['E0'])
        S.op('dve', lambda e, src=src, dst=dst: e.tensor_copy(out=dst[:, 16:32, :], in_=src[:, 16:32, ::-1]), reads=['E0'] + PK, writes=['E0'])
    t3a = AR.alloc("t3a", [128, 32, 16], F32)[:]; t3b = AR.alloc("t3b", [128, 32, 16], F32)[:]
    fb = lambda t: t[:].unsqueeze(2).to_broadcast([128, 32, 16])
    cmul(bbr[:], bbi[:], fb(fre), fb(fim), breT[:], bimT[:], t3a, t3b)
    TT(bsr[:], bbr[:], fb(rho), ALU.mult)
    TT(bsi[:], bbi[:], fb(rho), ALU.mult)
    del S.op
    for nm, tt_ in (("lamre", lamre), ("lamim", lamim), ("logdt", logdt), ("bre", breT), ("bim", bimT), ("cre", creT),
                    ("cim", cimT), ("dcol", dcol)):
        S.dma('sp', lambda e, nm=nm, tt_=tt_: e.dma_start(out=tt_[:], in_=din[nm]), writes=['E0'])
    n_head = 0
    for k_, op_ in enumerate(chain_ops):
        if op_[0] == 'act':
            n_head = k_ + 1
    for _ in range(n_head):
        S.op(*chain_ops.pop(0))
    iters = [(h, qc) for h in range(8) for qc in range(4)]
    NS = len(iters) * 9
    ob = 6

    def s_mm(s):
        it, kp = divmod(s, 9)
        h, qc = iters[it]
        kv = h // 4; hp = h // 2; h2 = h % 2
        qs = slice(qc * 512, (qc + 1) * 512)
        pr = s % 3
        for hf in range(2):
            kt = 2 * kp + hf
            S.op('pe', lambda e, kt=kt, hf=hf: e.matmul(PS[2 * pr + hf][:, :], lhsT=kTz[:, kv, h2, kt * 128:(kt + 1) * 128],
                                                      rhs=qT[:, hp, qs], start=True, stop=True),
                 reads=[], writes=[('ps', 2 * pr + hf)])

    def step(s):
        it, kp = divmod(s, 9)
        h, qc = iters[it]
        kv = h // 4; hp = h // 2; h2 = h % 2; ro = h2 * 64
        qs = slice(qc * 512, (qc + 1) * 512)
        pr = s % 3
        ot = osb[it % 2]
        ob = 6 + it % 2
        S.op('act', lambda e: e.activation(out=pT2[pr][:], in_=PSP[pr][:, :], func=AF.Exp, scale=0.125),
             reads=[('ps', 2 * pr), ('ps', 2 * pr + 1)], writes=[('pT', pr)])
        for hf in range(2):
            kt = 2 * kp + hf
            S.op('pe', lambda e, kt=kt, hf=hf: e.matmul(PS[ob][0:65, :], lhsT=Vaug[:, kt, kv, :], rhs=pT2[pr][:, hf * 512:(hf + 1) * 512],
                                                      start=(kt == 0), stop=(kt == 17)),
                 reads=[('pT', pr)], writes=[('ps', ob)])
        if kp == 1 and pending_fin:
            pending_fin[0][0]()
        if kp == 4 and pending_fin:
            pending_fin.pop()[1]()
        if kp == 8:
            S.op('dve', lambda e: e.tensor_copy(out=ot[0:65, :], in_=PS[ob][0:65, :]),
                 reads=[('ps', ob)], writes=[('osb', it % 2)])

            def fin_a():
                S.op('dve', lambda e: e.reciprocal(out=rsum[64:65, :], in_=ot[64:65, :]), reads=[('osb', it % 2)], writes=['rsum'])

            def fin():
                S.op('pe', lambda e: e.matmul(PS[ob][0:64, :], lhsT=ones_f[64:65, 0:64], rhs=rsum[64:65, :], start=True, stop=True),
                     reads=['rsum', 'ones_f'], writes=[('ps', ob)])
                S.op('dve', lambda e: e.tensor_tensor(out=tmpD[0:64, :], in0=ot[0:64, :], in1=PS[ob][0:64, :], op=ALU.mult),
                     reads=[('ps', ob), ('osb', it % 2)], writes=['tmpD'])
                S.op('dve', lambda e: e.tensor_tensor(out=ygT[ro:ro + 64, hp, qs], in0=tmpD[0:64, :], in1=sgT[0:64, h, qs], op=ALU.mult),
                     reads=['tmpD'] + [('sgT', h, tb) for tb in range(4)], writes=[('ygT', h, qc)])
            pending_fin.append((fin_a, fin))
    pending_fin = []
    s_mm(0)
    s_mm(1)
    for s in range(NS):
        if s + 2 < NS:
            s_mm(s + 2)
        step(s)
        if s >= 12 and 2 <= s % 9 <= 7:
            for _ in range(2):
                if chain_ops:
                    S.op(*chain_ops.pop(0))
    while chain_ops:
        S.op(*chain_ops.pop(0))
    while pending_fin:
        fa, fb = pending_fin.pop()
        fa(); fb()
    dump("ygT", ygT[:], [128, 4, NLAT], BF16, reads=[('ygT', h, qc) for h in range(8) for qc in range(4)])
    S.flush()
    AR.release(m_mid2)
    if stop == 'D':
        return nc, din, consts

    AR.hi = HI_WTM
    ygsT = nc.alloc_sbuf_tensor_at("ygsT_alias", [128, 4, NLAT], BF16, offset=AR.offs["utm"])
    m_E = AR.mark()
    rho_k = AR.alloc("rho", [128, 32], F32); phi_k = AR.alloc("phi", [128, 32], F32)
    pos = AR.alloc("pos", [128, 2, NCH], F32)
    CtRe = AR.alloc("CtRe", [128, 32, 128], BF16); nCtIm = AR.alloc("nCtIm", [128, 32, 128], BF16)
    GsTr = AR.alloc("GsTr", [128, 32, 128], BF16); GsTi = AR.alloc("GsTi", [128, 32, 128], BF16)
    WT = AR.alloc("WT", [128, 32, 2, 128], BF16)
    m_E0 = AR.mark()
    maskF = AR.alloc("maskF", [128, 128], F32); maskB = AR.alloc("maskB", [128, 128], F32)
    for nm, tt_ in (("maskF", maskF), ("maskB", maskB), ("pos", pos)):
        S.dma('sp', lambda e, nm=nm, tt_=tt_: e.dma_start(out=tt_[:], in_=din[nm]), writes=['E0'])
    S.op('dve', lambda e: e.tensor_copy(out=rho_k[:], in_=rho[:]), reads=['E0'], writes=['E0'])
    S.op('dve', lambda e: e.tensor_copy(out=phi_k[:], in_=phi[:]), reads=['E0'], writes=['E0'])
    m_E1 = AR.mark()
    Gsr = AR.alloc("Gsr", [128, 32, 128], BF16); Gsi = AR.alloc("Gsi", [128, 32, 128], BF16)
    Gpr = AR.alloc("Gpr", [128, 32, 128], BF16); Gpi = AR.alloc("Gpi", [128, 32, 128], BF16)
    T1 = AR.alloc("T1", [128, 2048], F32); T2 = AR.alloc("T2", [128, 2048], F32)
    t4a = T1[:].rearrange("p (q s h) -> p q s h", s=8, h=16); t4b = T2[:].rearrange("p (q s h) -> p q s h", s=8, h=16)
    P1_ = AR.alloc("P1", [128, 1024], F32); P2_ = AR.alloc("P2", [128, 1024], F32)
    p4a = P1_[:].rearrange("p (q s h) -> p q s h", s=8, h=16); p4b = P2_[:].rearrange("p (q s h) -> p q s h", s=8, h=16)

    def TTp(o, a, b, op):
        S.op('pool', lambda e: e.tensor_tensor(out=o, in0=a, in1=b, op=op), reads=['Ct'], writes=['Ct'])
    for qq in range(4):
        qs8 = slice(qq * 8, (qq + 1) * 8)
        v4p = lambda t, qs8=qs8: t[:, qs8, :].rearrange("p q (s h) -> p q s h", h=16)
        pb4p = lambda t, qs8=qs8: t[:, qs8, :].unsqueeze(3).to_broadcast([128, 8, 8, 16])
        hb4p = lambda t, qs8=qs8: t[:, qs8, :].unsqueeze(2).to_broadcast([128, 8, 8, 16])
        TTp(p4a, hb4p(creT), pb4p(PWsr), ALU.mult); TTp(p4b, hb4p(cimT), pb4p(PWsi), ALU.mult)
        TTp(v4p(CtRe), p4a, p4b, ALU.subtract)
        TTp(p4a, hb4p(creT), pb4p(PWsi), ALU.mult); TTp(p4b, hb4p(cimT), pb4p(PWsr), ALU.mult)
        S.op('pool', lambda e: e.tensor_scalar(out=p4a, in0=p4a, scalar1=-1.0, scalar2=1.0, op0=ALU.mult, op1=ALU.mult),
             reads=['Ct'], writes=['Ct'])
        TTp(v4p(nCtIm), p4a, p4b, ALU.subtract)
    for hq in range(2):
        qs_ = slice(hq * 16, (hq + 1) * 16)
        v4 = lambda t: t[:, qs_, :].rearrange("p q (s h) -> p q s h", h=16)
        pb4 = lambda t: t[:, qs_, :].unsqueeze(3).to_broadcast([128, 16, 8, 16])
        hb4 = lambda t: t[:, qs_, :].unsqueeze(2).to_broadcast([128, 16, 8, 16])
        cmul(v4(Gsr), v4(Gsi), pb4(NPsr), pb4(NPsi), hb4(bbr), hb4(bbi), t4a, t4b)
        cmul(v4(Gpr), v4(Gpi), pb4(NPsr), pb4(NPsi), hb4(bsr), hb4(bsi), t4a, t4b)
    for part, (Gp, GT) in enumerate(((Gpr, GsTr), (Gpi, GsTi))):
        for q0 in range(0, 32, 8):
            bk = (part * 4 + q0 // 8) % 4
            pbv = PS[bk][:].bitcast(BF16)
            for qq in range(8):
                S.op('pe', lambda e, Gp=Gp, q0=q0, qq=qq, pbv=pbv: e.transpose(out=pbv[:, qq * 128:(qq + 1) * 128], in_=Gp[:, q0 + qq, :],
                                                                           identity=ident_b[:]), reads=['E0'], writes=[('ps', bk)])
            S.op('act', lambda e, GT=GT, q0=q0, pbv=pbv: e.activation(out=GT[:, q0:q0 + 8, :], in_=pbv.rearrange("p (a b) -> p a b", b=128),
                                                                    func=AF.Copy), reads=[('ps', bk)], writes=['GsT'])
    tmpW = T1[:, 0:512].rearrange("p (a b) -> p a b", b=128)
    for q in range(32):
        dr = q // 16
        mk = maskF if dr == 0 else maskB
        for g2 in range(2):
            bk = 4 + (q % 2) * 2 + g2
            rs = slice(g2 * 64, (g2 + 1) * 64)
            S.op('pe', lambda e, q=q, rs=rs, bk=bk: e.matmul(PS[bk][:, 0:128], lhsT=Gsr[rs, q, :], rhs=CtRe[rs, q, :], start=True, stop=False),
                 reads=['E0', 'Ct'], writes=[('ps', bk)])
            S.op('pe', lambda e, q=q, rs=rs, bk=bk: e.matmul(PS[bk][:, 0:128], lhsT=Gsi[rs, q, :], rhs=nCtIm[rs, q, :], start=False, stop=True),
                 reads=['E0', 'Ct'], writes=[('ps', bk)])
            if dr == 1:
                S.op('dve', lambda e, q=q, g2=g2, bk=bk, mk=mk: e.tensor_tensor(out=WT[:, q, g2, :], in0=PS[bk][:, 0:128], in1=mk[:], op=ALU.mult),
                     reads=[('ps', bk), 'E0'], writes=['E0'])
            else:
                g = 2 * (q % 16) + g2
                S.op('dve', lambda e, g2=g2, bk=bk, mk=mk: e.tensor_tensor(out=tmpW[:, g2, :], in0=PS[bk][:, 0:128], in1=mk[:], op=ALU.mult),
                     reads=[('ps', bk), 'E0'], writes=['E0'])
                D1(lambda e, q=q, g2=g2, g=g: e.scalar_tensor_tensor(out=WT[:, q, g2, :], in0=ident_f[:], scalar=dcol[:, g:g + 1],
                                                                   in1=tmpW[:, g2, :], op0=ALU.mult, op1=ALU.add))
    dump("WT", WT[:], [128, 32, 2, 128], BF16, reads=['E0'])
    dump("GsTr", GsTr[:], [128, 32, 128], BF16, reads=['GsT', 'E0'])
    dump("CtRe", CtRe[:], [128, 32, 128], BF16, reads=['E0'])
    S.flush()
    AR.release(m_E0)
    AR.hi = 229344
    Xo = AR.alloc("Xo", [128, 2, 8, 512], BF16)
    m_Xo = AR.mark()
    U = AR.alloc("U", [128, 32, NCH], BF16)
    m_E1 = AR.mark()
    sel = AR.alloc("sel", [128, 8, 8, 128], BF16)
    X2 = AR.alloc("X2", [128, 3, 32, 128], BF16)
    S.dma('sp', lambda e: e.dma_start(out=sel[:], in_=din["sel"]), writes=['sel'])
    for ct in range(3):
        nj = 8 if ct < 2 else 2
        for hf in range(2):
            for s4 in range(4):
                s_ = hf * 4 + s4
                for j in range(nj):
                    S.op('pe', lambda e, ct=ct, s4=s4, s_=s_, j=j, nj=nj: e.matmul(PS[s4][:, :], lhsT=sel[:, j, s_, :], rhs=utm[:, 8 * ct + j, :],
                                                                                 start=(j == 0), stop=(j == nj - 1)),
                         reads=['sel'], writes=[('ps', s4)])
                S.op('act' if s4 % 2 == 0 else 'dve',
                     (lambda e, ct=ct, s4=s4, s_=s_: e.activation(out=X2[:, ct, :, s_ * 16:(s_ + 1) * 16],
                                                                  in_=PS[s4][:, :].rearrange("p (g h) -> p g h", h=16), func=AF.Copy))
                     if s4 % 2 == 0 else
                     (lambda e, ct=ct, s4=s4, s_=s_: e.tensor_copy(out=X2[:, ct, :, s_ * 16:(s_ + 1) * 16],
                                                                   in_=PS[s4][:, :].rearrange("p (g h) -> p g h", h=16))),
                     reads=[('ps', s4)], writes=[('X2', ct, s_)])
    for g0 in range(0, 32, 3):
        ng = min(3, 32 - g0)
        bk = 4 + (g0 // 3) % 4
        pbv = PS[bk][:].bitcast(BF16)
        for gi in range(ng):
            g = g0 + gi
            for ct in range(3):
                nr = 128 if ct < 2 else 32
                S.op('pe', lambda e, g=g, gi=gi, ct=ct, nr=nr, pbv=pbv: e.transpose(
                    out=pbv[:, gi * NCH + ct * 128:gi * NCH + ct * 128 + nr], in_=X2[0:nr, ct, g, :], identity=ident_b[0:nr, 0:nr]),
                    reads=[('X2', ct, s_) for s_ in range(8)], writes=[('ps', bk)])
        S.op('act' if (g0 // 3) % 2 == 0 else 'dve',
             (lambda e, g0=g0, ng=ng, pbv=pbv: e.activation(out=U[:, g0:g0 + ng, :], in_=pbv[:, 0:ng * NCH].rearrange("p (a b) -> p a b", b=NCH),
                                                            func=AF.Copy)) if (g0 // 3) % 2 == 0 else
             (lambda e, g0=g0, ng=ng, pbv=pbv: e.tensor_copy(out=U[:, g0:g0 + ng, :], in_=pbv[:, 0:ng * NCH].rearrange("p (a b) -> p a b", b=NCH))),
             reads=[('ps', bk)], writes=['U'])
    dump("U", U[:], [128, 32, NCH], BF16, reads=['U'])
    S.flush()
    AR.release(m_E1)
    if stop == 'E1':
        return nc, din, consts
    Ere_ = [AR.alloc(f"Ere{i}", [128, 4, NCH], F32) for i in range(2)]
    Eim_ = [AR.alloc(f"Eim{i}", [128, 4, NCH], F32) for i in range(2)]
    ang = AR.alloc("ang", [128, 4, NCH], F32); kfi = AR.alloc("kfi", [128, 4, NCH], I32)
    Rre = AR.alloc("Rre", [128, 2, 4, NCH], BF16); Rim = AR.alloc("Rim", [128, 2, 4, NCH], BF16)
    Xr = AR.alloc("Xr", [128, NCH], F32); Xi = AR.alloc("Xi", [128, NCH], F32)
    Wa = AR.alloc("Wa", [128, NCH], F32); Wb = AR.alloc("Wb", [128, NCH], F32)
    Pa = AR.alloc("Pa", [128, NCH], F32); Pb = AR.alloc("Pb", [128, NCH], F32)
    Zr_ = [AR.alloc(f"Zr{i}", [128, NCH + 2], F32) for i in range(2)]
    Zi_ = [AR.alloc(f"Zi{i}", [128, NCH + 2], F32) for i in range(2)]
    Ysb_ = [nc.alloc_sbuf_tensor_at(f"Ysb{i}", [128, 256], BF16, offset=AR.offs["utm"] + 512 * i) for i in range(2)]
    C1 = 6.28125
    C2 = 2 * math.pi - C1
    for zz in Zr_ + Zi_:
        S.op('dve', lambda e, zz=zz: e.memset(zz[:], 0.0), writes=[('Z', 0), ('Z', 1)])
    gcount = [0]

    def build_tables(qt, dr):
        q0 = dr * 16 + qt * 4
        Ere = Ere_[dr]; Eim = Eim_[dr]
        EK = ('E', dr)
        S.op('pool', lambda e: e.tensor_tensor(out=ang[:], in0=pos[:, dr, :].unsqueeze(1).to_broadcast([128, 4, NCH]),
                                               in1=phi_k[:, q0:q0 + 4].unsqueeze(2).to_broadcast([128, 4, NCH]), op=ALU.mult),
             reads=[], writes=['ang'])
        S.op('dve', lambda e: e.tensor_scalar(out=kfi[:], in0=ang[:], scalar1=1.0 / (2 * math.pi), scalar2=None, op0=ALU.mult),
             reads=['ang'], writes=['kfi'])
        S.op('dve', lambda e: e.scalar_tensor_tensor(out=ang[:], in0=kfi[:], scalar=-C1, in1=ang[:], op0=ALU.mult, op1=ALU.add),
             reads=['kfi', 'ang'], writes=['ang'])
        S.op('dve', lambda e: e.scalar_tensor_tensor(out=ang[:], in0=kfi[:], scalar=-C2, in1=ang[:], op0=ALU.mult, op1=ALU.add),
             reads=['kfi', 'ang'], writes=['ang'])
        S.op('dve', lambda e: e.tensor_scalar(out=ang[:], in0=ang[:], scalar1=3.141592, scalar2=-3.141592, op0=ALU.min, op1=ALU.max),
             reads=['ang'], writes=['ang'])
        S.op('act', lambda e: e.activation(out=Eim[:], in_=ang[:], func=AF.Sin, scale=-1.0), reads=['ang'], writes=[EK])
        S.op('dve', lambda e: e.scalar_tensor_tensor(out=ang[:], in0=ang[:], scalar=-1.0, in1=ang[:], op0=ALU.mult, op1=ALU.max),
             reads=['ang'], writes=['ang'])
        S.op('act', lambda e: e.activation(out=Ere[:], in_=ang[:], func=AF.Sin, scale=-1.0, bias=halfpi2[:]), reads=['ang'], writes=[EK])


    def do_gl(qt, dr, gl):
        q0 = dr * 16 + qt * 4
        Ere = Ere_[dr]; Eim = Eim_[dr]
        EK = ('E', dr)
        q = q0 + gl
        gp = qt * 4 + gl
        par = gcount[0] % 2
        gcount[0] += 1
        Zr = Zr_[par]; Zi = Zi_[par]
        ZK = ('Z', par)
        bre_, bim_ = par * 2, par * 2 + 1
        for g2 in range(2):
            g = 2 * gp + g2
            S.op('pe', lambda e, g=g, g2=g2: e.matmul(PS[bre_][g2 * 64:(g2 + 1) * 64, 0:NCH], lhsT=GsTr[:, q, g2 * 64:(g2 + 1) * 64],
                                                    rhs=U[:, g, :], start=True, stop=True), reads=[], writes=[('ps', bre_)])
            S.op('pe', lambda e, g=g, g2=g2: e.matmul(PS[bim_][g2 * 64:(g2 + 1) * 64, 0:NCH], lhsT=GsTi[:, q, g2 * 64:(g2 + 1) * 64],
                                                    rhs=U[:, g, :], start=True, stop=True), reads=[], writes=[('ps', bim_)])
        Er = Ere[:, gl, :]; Ei = Eim[:, gl, :]
        Vr = PS[bre_][:, 0:NCH]; Vi = PS[bim_][:, 0:NCH]

        def V(fn, rd, wr):
            S.op('dve', fn, reads=rd, writes=wr)
        V(lambda e: e.tensor_tensor(out=Wa[:], in0=Vr, in1=Er, op=ALU.mult), [('ps', bre_), EK], ['Wa'])
        V(lambda e: e.tensor_tensor(out=Wb[:], in0=Vi, in1=Ei, op=ALU.mult), [('ps', bim_), EK], ['Wb'])
        V(lambda e: e.tensor_tensor(out=Xr[:], in0=Wa[:], in1=Wb[:], op=ALU.subtract), ['Wa', 'Wb'], ['Xr'])
        V(lambda e: e.tensor_tensor(out=Wa[:], in0=Vr, in1=Ei, op=ALU.mult), [('ps', bre_), EK], ['Wa'])
        V(lambda e: e.tensor_tensor(out=Wb[:], in0=Vi, in1=Er, op=ALU.mult), [('ps', bim_), EK], ['Wb'])
        V(lambda e: e.tensor_tensor(out=Xi[:], in0=Wa[:], in1=Wb[:], op=ALU.add), ['Wa', 'Wb'], ['Xi'])
        rb = rho_k[:, q:q + 1]
        for (Xx, Zz, nmx) in ((Xr, Zr, 'Xr'), (Xi, Zi, 'Xi')):
            if dr == 0:
                V(lambda e, Xx=Xx, Zz=Zz: e.tensor_tensor_scan(out=Zz[:, 257:289], data0=rb.to_broadcast([128, 32]), data1=Xx[:, 256:288],
                                                               initial=0.0, op0=ALU.mult, op1=ALU.add), [nmx], [ZK])
                V(lambda e, Zz=Zz: e.tensor_copy(out=Zz[:, 0:1], in_=Zz[:, 288:289]), [ZK], [ZK])
                V(lambda e, Xx=Xx, Zz=Zz: e.tensor_tensor_scan(out=Zz[:, 1:257], data0=rb.to_broadcast([128, 256]), data1=Xx[:, 0:256],
                                                               initial=Zz[:, 288:289], op0=ALU.mult, op1=ALU.add), [nmx, ZK], [ZK])
            else:
                V(lambda e, Zz=Zz: e.memset(Zz[:, 288:290], 0.0), [], [ZK])
                V(lambda e, Xx=Xx, Zz=Zz: e.tensor_tensor_scan(out=Zz[:, 0:288][:, ::-1], data0=rb.to_broadcast([128, NCH]), data1=Xx[:, ::-1],
                                                               initial=0.0, op0=ALU.mult, op1=ALU.add), [nmx, ZK], [ZK])
        zo = 0 if dr == 0 else 1
        Zpr = Zr[:, zo:zo + NCH]; Zpi = Zi[:, zo:zo + NCH]
        RK = ('Rw', dr, gl)

        def P(fn, rd, wr):
            S.op('pool', fn, reads=rd, writes=wr)
        P(lambda e: e.tensor_tensor(out=Pa[:], in0=Er, in1=Zpr, op=ALU.mult), [EK, ZK], ['Pa'])
        P(lambda e: e.tensor_tensor(out=Pb[:], in0=Ei, in1=Zpi, op=ALU.mult), [EK, ZK], ['Pb'])
        P(lambda e: e.tensor_tensor(out=Rre[:, dr, gl, :], in0=Pa[:], in1=Pb[:], op=ALU.add), ['Pa', 'Pb'], [RK])
        P(lambda e: e.tensor_tensor(out=Pa[:], in0=Er, in1=Zpi, op=ALU.mult), [EK, ZK], ['Pa'])
        P(lambda e: e.tensor_tensor(out=Pb[:], in0=Ei, in1=Zpr, op=ALU.mult), [EK, ZK], ['Pb'])
        P(lambda e: e.tensor_tensor(out=Rim[:, dr, gl, :], in0=Pa[:], in1=Pb[:], op=ALU.subtract), ['Pa', 'Pb'], [RK])
        if dr == 0:
            P(lambda e: e.memset(Rre[:, 0, gl, 256:257], 0.0), [], [RK])
            P(lambda e: e.memset(Rim[:, 0, gl, 256:257], 0.0), [], [RK])

    def y_group(qt, gl):
        if True:
            gp = qt * 4 + gl
            for g2 in range(2):
                g = 2 * gp + g2
                yb = 4 + g % 2
                Ysb = Ysb_[g % 2]; YK = ('Ysb', g % 2)
                rs = slice(g2 * 64, (g2 + 1) * 64)
                k = 0
                for dr in range(2):
                    q = dr * 16 + gp
                    for (lt, rh) in ((CtRe[rs, q, :], Rre[rs, dr, gl, :]), (nCtIm[rs, q, :], Rim[rs, dr, gl, :]), (WT[:, q, g2, :], U[:, g, :])):
                        S.op('pe', lambda e, lt=lt, rh=rh, k=k, yb=yb: e.matmul(PS[yb][:, 0:NCH], lhsT=lt, rhs=rh, start=(k == 0), stop=(k == 5)),
                             reads=[('Rw', dr, gl)], writes=[('ps', yb)])
                        k += 1
                S.op('act', lambda e, yb=yb, Ysb=Ysb: e.activation(out=Ysb[:], in_=PS[yb][:, 0:256], func=AF.Copy), reads=[('ps', yb)], writes=[YK])
                tb_ = 6 + g % 2
                ptv = PS[tb_][:].bitcast(BF16)
                for ct in range(2):
                    S.op('pe', lambda e, ct=ct, ptv=ptv, Ysb=Ysb: e.transpose(out=ptv[:, ct * 128:(ct + 1) * 128], in_=Ysb[:, ct * 128:(ct + 1) * 128],
                                                                     identity=ident_b[:]), reads=[YK], writes=[('ps', tb_)])
                S.op('dve', lambda e, g=g, ptv=ptv: e.tensor_copy(out=Xo[:, :, :, g * 16:(g + 1) * 16],
                                                                  in_=ptv[:, 0:256].rearrange("p (c t h) -> p c t h", c=2, h=16)),
                     reads=[('ps', tb_)], writes=[('Xo', g)])


    halfpi2 = AR.alloc("halfpi2", [128, 1], F32)
    S.op('dve', lambda e: e.memset(halfpi2[:], math.pi / 2), writes=[('E', 0), ('E', 1)])
    seq_ = [(qt, dr) for qt in range(4) for dr in range(2)]
    build_tables(*seq_[0])
    for k_, (qt, dr) in enumerate(seq_):
        for gl in range(4):
            if dr == 0 and qt > 0:
                y_group(qt - 1, gl)
            do_gl(qt, dr, gl)
            if gl == 1 and k_ + 1 < len(seq_):
                build_tables(*seq_[k_ + 1])
    for gl in range(4):
        y_group(3, gl)
    dump("Xo", Xo[:], [128, 2, 8, 512], BF16, reads=[('Xo', g) for g in range(32)])
    S.flush()
    AR.release(m_Xo)
    if stop == 'E2':
        return nc, din, consts
    selT = AR.alloc("selT", [128, 8, 512], BF16)
    zT = AR.alloc("zT", [128, 4, NLAT], BF16)
    wglu = AR.alloc("wglu", [128, 4, 512], BF16)
    wgb = AR.alloc("wgb", [128, 8, 512], BF16)
    bglu = AR.alloc("bglu", [128, 4], F32)
    sgl4 = [AR.alloc(f"sgl{i}", [128, 512], F32) for i in range(4)]
    sgb4 = [AR.alloc(f"sgb{i}", [128, 512], F32) for i in range(4)]
    tE = AR.alloc("tE", [128, 512], F32)
    S.dma('sp', lambda e: e.dma_start(out=selT[:], in_=din["selT"]), writes=['selT'])
    S.dma('sp', lambda e: e.dma_start(out=bglu[:], in_=din["b_gluT"]), writes=['bglu'])
    for ci in range(4):
        S.dma('pool', lambda e, ci=ci: e.dma_start(out=wglu[:, ci, :], in_=din["w_glu"][ci * 128:(ci + 1) * 128, :]), writes=['wglu'])
    for kt in range(8):
        S.dma('pool', lambda e, kt=kt: e.dma_start(out=wgb[:, kt, :], in_=din["w_in"][kt * 128:(kt + 1) * 128, 1792:2304]), writes=['wgb'])

    assert AR.offs["nCtIm"] == AR.offs["CtRe"] + 8192 and AR.offs["GsTi"] == AR.offs["GsTr"] + 8192
    wma = nc.alloc_sbuf_tensor_at("wma_pf", [128, 8, 1024], BF16, offset=AR.offs["CtRe"])
    wmb = nc.alloc_sbuf_tensor_at("wmb_pf", [128, 8, 1024], BF16, offset=AR.offs["GsTr"])
    wbra = nc.alloc_sbuf_tensor_at("wbra_pf", [128, 4, 1024], BF16, offset=AR.offs["WT"])
    wbrb = nc.alloc_sbuf_tensor_at("wbrb_pf", [128, 4, 1024], BF16, offset=AR.offs["WT"] + 8192)
    w_slot_end = AR.offs["WT"] + 16384
    for a in range(4):
        S.dma('pool', lambda e, a=a: e.dma_start(out=wbra[:, a, :], in_=din["w_bra"][a * 128:(a + 1) * 128, :]), writes=['wbra'])
        S.dma('pool', lambda e, a=a: e.dma_start(out=wbrb[:, a, :], in_=din["w_brb"][a * 128:(a + 1) * 128, :]), writes=['wbrb'])
    for kt in range(8):
        S.dma('pool', lambda e, kt=kt: e.dma_start(out=wma[:, kt, :], in_=din["w_in"][kt * 128:(kt + 1) * 128, 2304:3328]), writes=['wma'])
        S.dma('pool', lambda e, kt=kt: e.dma_start(out=wmb[:, kt, :], in_=din["w_in"][kt * 128:(kt + 1) * 128, 3328:4352]), writes=['wmb'])

    def e3(tb):
        ts_ = slice(tb * 512, (tb + 1) * 512)
        rows = slice((tb % 2) * 64, (tb % 2) * 64 + 64)
        ct = tb // 2
        for ci in range(4):
            b = ci % 2
            for s_ in range(8):
                S.op('pe', lambda e, ci=ci, s_=s_, b=b: e.matmul(PS[b][:, :], lhsT=Xo[rows, ct, s_, ci * 128:(ci + 1) * 128], rhs=selT[rows, s_, :],
                                                                 start=(s_ == 0), stop=(s_ == 7)), reads=['selT'], writes=[('ps', b)])
            S.op('act', lambda e, ci=ci, b=b: e.activation(out=zT[:, ci, ts_], in_=PS[b][:, :], func=AF.Gelu_apprx_tanh),
                 reads=[('ps', b)], writes=[('zT', ci)])
        for co in range(4):
            b1 = 2 + co % 2
            for ci in range(4):
                S.op('pe', lambda e, ci=ci, co=co, b1=b1: e.matmul(PS[b1][:, :], lhsT=wglu[:, ci, co * 128:(co + 1) * 128], rhs=zT[:, ci, ts_],
                                                                   start=(ci == 0), stop=(ci == 3)), reads=['wglu', ('zT', ci)], writes=[('ps', b1)])
            S.op('act', lambda e, co=co, b1=b1: e.activation(out=sgl4[co][:], in_=PS[b1][:, :], func=AF.Sigmoid, bias=bglu[:, co:co + 1]),
                 reads=[('ps', b1), 'bglu'], writes=[('sgl', co)])
        for co in range(4):
            b2 = 4 + co % 2
            for kt in range(8):
                S.op('pe', lambda e, kt=kt, co=co, b2=b2: e.matmul(PS[b2][:, :], lhsT=wgb[:, kt, co * 128:(co + 1) * 128], rhs=xnT[:, kt, ts_],
                                                                   start=(kt == 0), stop=(kt == 7)), reads=['wgb'], writes=[('ps', b2)])
            S.op('act', lambda e, co=co, b2=b2: e.activation(out=sgb4[co][:], in_=PS[b2][:, :], func=AF.Silu), reads=[('ps', b2)], writes=[('sgb', co)])
        for co in range(4):
            S.op('dve', lambda e, co=co: e.tensor_tensor(out=tE[:], in0=zT[:, co, ts_], in1=sgl4[co][:], op=ALU.mult),
                 reads=[('zT', co), ('sgl', co)], writes=['tE'])
            S.op('dve', lambda e, co=co: e.tensor_tensor(out=ygsT[:, co, ts_], in0=tE[:], in1=sgb4[co][:], op=ALU.mult),
                 reads=['tE', ('sgb', co)], writes=[('ygsT', co, tb)])
    for tb in range(4):
        e3(tb)
    dump("ygsT", ygsT[:], [128, 4, NLAT], BF16, reads=[('ygsT', co, tb) for co in range(4) for tb in range(4)])
    S.flush()
    AR.release(m_E)
    if stop == 'E3':
        return nc, din, consts

    AR.cur = max(AR.cur, w_slot_end)
    mixT = AR.alloc("mixT", [128, 8, NLAT], BF16)
    m_F = AR.mark()
    sa = AR.alloc("sa", [128, 512], F32); sb_ = AR.alloc("sb_", [128, 512], F32)
    f1 = AR.alloc("f1", [128, 512], F32); f2 = AR.alloc("f2", [128, 512], F32)

    def fstep(tb, j, n):
        ts_ = slice(tb * 512, (tb + 1) * 512); js = slice(j * 128, (j + 1) * 128)
        b0 = (n % 2) * 4
        for a in range(4):
            S.op('pe', lambda e, a=a: e.matmul(PS[b0][:, :], lhsT=wbra[:, a, js], rhs=ygT[:, a, ts_], start=(a == 0), stop=(a == 3)),
                 reads=['wbra'], writes=[('ps', b0)])
        for a in range(4):
            S.op('pe', lambda e, a=a: e.matmul(PS[b0 + 1][:, :], lhsT=wbrb[:, a, js], rhs=ygsT[:, a, ts_], start=(a == 0), stop=(a == 3)),
                 reads=['wbrb'], writes=[('ps', b0 + 1)])
        for kt in range(8):
            S.op('pe', lambda e, kt=kt: e.matmul(PS[b0 + 2][:, :], lhsT=wma[:, kt, js], rhs=xnT[:, kt, ts_], start=(kt == 0), stop=(kt == 7)),
                 reads=['wma'], writes=[('ps', b0 + 2)])
        for kt in range(8):
            S.op('pe', lambda e, kt=kt: e.matmul(PS[b0 + 3][:, :], lhsT=wmb[:, kt, js], rhs=xnT[:, kt, ts_], start=(kt == 0), stop=(kt == 7)),
                 reads=['wmb'], writes=[('ps', b0 + 3)])
        S.op('act', lambda e: e.activation(out=sa[:], in_=PS[b0 + 2][:, :], func=AF.Sigmoid), reads=[('ps', b0 + 2)], writes=['sa'])
        S.op('act', lambda e: e.activation(out=sb_[:], in_=PS[b0 + 3][:, :], func=AF.Sigmoid), reads=[('ps', b0 + 3)], writes=['sb_'])
        S.op('dve', lambda e: e.tensor_tensor(out=f1[:], in0=PS[b0][:, :], in1=sa[:], op=ALU.mult), reads=[('ps', b0), 'sa'], writes=['f1'])
        S.op('dve', lambda e: e.tensor_tensor(out=f2[:], in0=PS[b0 + 1][:, :], in1=sb_[:], op=ALU.mult), reads=[('ps', b0 + 1), 'sb_'], writes=['f2'])
        S.op('pool', lambda e: e.tensor_tensor(out=mixT[:, j, ts_], in0=f1[:], in1=f2[:], op=ALU.add), reads=['f1', 'f2'], writes=[('mixT', j, tb)])
    woutg = AR.alloc("woutg", [128, 8, 1024], BF16)
    wo = [AR.alloc(f"wo{i}", [128, 1024], F32) for i in range(2)]
    fg = AR.alloc("fg", [128, 1024], F32)
    xg = [AR.alloc(f"xg{i}", [128, 1024], F32) for i in range(3)]
    junkG = AR.alloc("junkG", [128, 1024], BF16)
    ssG = AR.alloc("ssG", [128, 16], F32)
    neghalf = AR.alloc("neghalf", [128, 1], F32)
    S.op('pool', lambda e: e.memset(neghalf[:], -0.5), writes=['neghalf'])
    S.dma('sp', lambda e: e.dma_start(out=fg[:], in_=din["fg_bc"]), writes=['fg'])
    for kt in range(8):
        w_ = wo[kt % 2]
        S.dma('sp', lambda e, kt=kt, w_=w_: e.dma_start(out=w_[:], in_=din["w_out"][kt * 128:(kt + 1) * 128, :]), writes=[('wo', kt % 2)])
        S.op('dve', lambda e, kt=kt, w_=w_: e.tensor_tensor(out=woutg[:, kt, :], in0=w_[:], in1=gate_bc[:], op=ALU.mult),
             reads=[('wo', kt % 2)], writes=[('woutg', kt)])

    def g_a(i, nb):
        xt_ = xg[i % 3]
        tb = i // 4
        S.dma('sp', lambda e: e.dma_start(out=xt_[:], in_=x_d[i * 128:(i + 1) * 128, :]), writes=[('xg', i % 3)])
        for c2 in range(2):
            b = nb + c2
            for kt in range(8):
                S.op('pe', lambda e, kt=kt, b=b, c2=c2: e.matmul(PS[b][:, :], lhsT=mixT[:, kt, i * 128:(i + 1) * 128], rhs=woutg[:, kt, c2 * 512:(c2 + 1) * 512],
                                                                start=(kt == 0), stop=(kt == 7)), reads=[('woutg', kt), ('mixT', kt, tb)], writes=[('ps', b)])
            S.op('dve', lambda e, b=b, c2=c2: e.tensor_tensor(out=xt_[:, c2 * 512:(c2 + 1) * 512], in0=PS[b][:, :], in1=xt_[:, c2 * 512:(c2 + 1) * 512], op=ALU.add),
                 reads=[('ps', b), ('xg', i % 3)], writes=[('xg', i % 3)])
        S.op('act', lambda e: e.activation(out=junkG[:], in_=xt_[:], func=AF.Square, accum_out=ssG[:, i:i + 1]),
             reads=[('xg', i % 3)], writes=['junkG', ('ssG', i)])

    def g_b(i):
        S.op('dve', lambda e: e.tensor_scalar(out=ssG[:, i:i + 1], in0=ssG[:, i:i + 1], scalar1=1.0 / D, scalar2=EPS, op0=ALU.mult, op1=ALU.add),
             reads=[('ssG', i)], writes=[('ssG', i)])
        S.op('pool', lambda e: e.tensor_tensor(out=ssG[:, i:i + 1], in0=ssG[:, i:i + 1], in1=neghalf[:, 0:1], op=ALU.pow),
             reads=[('ssG', i), 'neghalf'], writes=[('ssG', i)])

    def g_c(i):
        xt_ = xg[i % 3]
        S.op('dve', lambda e: e.scalar_tensor_tensor(out=xt_[:], in0=xt_[:], scalar=ssG[:, i:i + 1], in1=fg[:], op0=ALU.mult, op1=ALU.mult),
             reads=[('ssG', i), ('xg', i % 3), 'fg'], writes=[('xg', i % 3)])
        S.dma('sp', lambda e: e.dma_start(out=out_d[i * 128:(i + 1) * 128, :], in_=xt_[:]), reads=[('xg', i % 3)])

    def g_pipe(i, nb):
        if i < 16:
            g_a(i, nb)
        if 0 <= i - 1 < 16:
            g_b(i - 1)
        if 0 <= i - 2 < 16:
            g_c(i - 2)
    n = 0
    gq = []
    for tb in range(4):
        for j in range(8):
            fstep(tb, j, n)
            if gq and j % 2 == 1:
                g_pipe(gq.pop(0), ((n + 1) % 2) * 4 + 2)
            n += 1
        gq += list(range(4 * tb, 4 * tb + 4))
    k_ = 0
    for i in gq + [16, 17]:
        g_pipe(i, (k_ % 2) * 4 + 2)
        k_ += 1
    S.flush()

    nc._dbg_out = dbg_out
    nc._arena_peak = AR.peak
    return nc, din, consts


def _run(inputs, dbg=(), stop=None):
    nc, din, consts = build(dbg, stop)
    shared = _layout_shared(inputs)
    shared.update(consts)
    x = np.asarray(inputs["x"], np.float32); ctx = np.asarray(inputs["ctx"], np.float32)
    c = np.asarray(inputs["c"], np.float32); c_ctx = np.asarray(inputs["c_ctx"], np.float32)
    in_maps = []
    for b in range(8):
        m = dict(shared)
        m["x"] = np.ascontiguousarray(x[b]); m["ctx"] = np.ascontiguousarray(ctx[b])
        cc = np.stack([c[b], c_ctx], axis=-1)
        m["cT"] = np.ascontiguousarray(cc.reshape(8, 128, 2).transpose(1, 0, 2))
        in_maps.append(m)
    res = run_bass_kernel_spmd(nc, in_maps, core_ids=list(range(8)))
    return res


def kernel(**inputs):
    res = _run(inputs)
    return np.stack([np.asarray(r["out"], np.float32) for r in res.results], axis=0)
```
